# Optimizing a Trainium2 kernel written in Bass

```python
import jax, jax.numpy as jnp
from jax import lax
import numpy as np

D_MODEL = 1024
BATCH = 8
SEQ = 2048
DEPTH = 1
DEC_BATCH = 128
DEC_SEQ = 4
PAST_LEN = 16384
PAGE_SIZE = 128

C_A = D_MODEL
C_B = D_MODEL
CONV_A_W = 31
CHUNK = 128
H_B = 8
HD_B = C_B // H_B
D_FF = ((8 * D_MODEL // 3 + 127) // 128) * 128
CONV_F_W = 3
EPS = 1e-6
N_IN = 2 * C_A + 2 * C_B + 2 * D_MODEL

kernel_name = "hybrid_conformer_conv_chunk_gmlp_convffn_step"


def _rmsnorm(x, g):
    xf = x.astype(jnp.float32)
    y = xf * lax.rsqrt(jnp.mean(xf * xf, axis=-1, keepdims=True) + EPS)
    return (y * g.astype(jnp.float32)).astype(x.dtype)


def _layernorm(x, g, b):
    xf = x.astype(jnp.float32)
    mu = jnp.mean(xf, axis=-1, keepdims=True)
    xc = xf - mu
    var = jnp.mean(xc * xc, axis=-1, keepdims=True)
    y = xc * lax.rsqrt(var + EPS) * g.astype(jnp.float32) + b.astype(jnp.float32)
    return y.astype(x.dtype)


def _causal_dwconv(x_full, w, b):
    c = x_full.shape[-1]
    out = lax.conv_general_dilated(
        x_full, w[:, None, :].astype(x_full.dtype), window_strides=(1,), padding='VALID',
        dimension_numbers=('NWC', 'WIO', 'NWC'), feature_group_count=c)
    return out + b.astype(out.dtype)


def _chunk_spatial_mix(v, w_s, b_s):
    n, t, c = v.shape
    L = min(t, CHUNK)
    t_pad = -(-t // L) * L
    if t_pad != t:
        v = jnp.pad(v, ((0, 0), (0, t_pad - t), (0, 0)))
    nc = t_pad // L
    vh = v.reshape(n, nc, L, H_B, HD_B)
    mask = jnp.tril(jnp.ones((L, L), dtype=bool))
    w = jnp.where(mask[None], w_s[:, :L, :L], 0.0).astype(v.dtype)
    bias = jnp.transpose(b_s[:, :L])[None, None, :, :, None].astype(v.dtype)
    s = jnp.einsum('hts,ncshd->ncthd', w, vh) + bias
    return s.reshape(n, t_pad, c)[:, :t]


def _token_mixers(xn, conv_prev, w_in, dw_a, b_dw_a, ln_a_g, ln_a_b, w_a_out,
                  ln_b_g, ln_b_b, w_s, b_s, w_b_out, w_out):
    proj = xn @ w_in.astype(xn.dtype)
    pa, pb, pg = jnp.split(proj, [2 * C_A, 2 * C_A + 2 * C_B], axis=-1)
    a_val, a_gate = jnp.split(pa, 2, axis=-1)
    glu = a_val * jax.nn.sigmoid(a_gate)
    glu_full = jnp.concatenate([conv_prev.astype(glu.dtype), glu], axis=1)
    conv_new = glu_full[:, -(CONV_A_W - 1):]
    ca = _causal_dwconv(glu_full, dw_a, b_dw_a)
    y_a = jax.nn.silu(_layernorm(ca, ln_a_g, ln_a_b)) @ w_a_out.astype(xn.dtype)
    z = jax.nn.gelu(pb, approximate=False)
    u, v = jnp.split(z, 2, axis=-1)
    v_n = _layernorm(v, ln_b_g, ln_b_b)
    y_b = (u * _chunk_spatial_mix(v_n, w_s, b_s)) @ w_b_out.astype(xn.dtype)
    g_a, g_b = jnp.split(jax.nn.sigmoid(pg), 2, axis=-1)
    out = (g_a * y_a + g_b * y_b) @ w_out.astype(xn.dtype)
    return out, conv_new, v_n


def _conv_ffn(xn, ffn_prev, w_up, dw_f, b_dw_f, w_down):
    h = xn @ w_up.astype(xn.dtype)
    a, b = jnp.split(h, 2, axis=-1)
    a_full = jnp.concatenate([ffn_prev.astype(a.dtype), a], axis=1)
    ffn_new = a_full[:, -(CONV_F_W - 1):]
    ac = _causal_dwconv(a_full, dw_f, b_dw_f)
    return (jax.nn.gelu(ac, approximate=False) * b) @ w_down.astype(xn.dtype), ffn_new


def setup_inputs(seed: int = 0) -> dict:
    key = jax.random.key(seed)
    ks = jax.random.split(key, 24)
    f32 = jnp.float32
    nrm = lambda k, shape, s: jax.random.normal(k, shape, f32) * s
    return {
        "x_prompt": nrm(ks[0], (BATCH, SEQ, D_MODEL), 1.0),
        "x_sample": nrm(ks[1], (DEC_BATCH, DEC_SEQ, D_MODEL), 1.0),
        "state_conv_a": nrm(ks[2], (DEPTH, DEC_BATCH, CONV_A_W - 1, C_A), 0.5),
        "state_ffn_conv": nrm(ks[3], (DEPTH, DEC_BATCH, CONV_F_W - 1, D_FF), 1.0),
        "g_mix": 1.0 + nrm(ks[4], (DEPTH, D_MODEL), 0.02),
        "w_in": nrm(ks[5], (DEPTH, D_MODEL, N_IN), D_MODEL ** -0.5),
        "dw_a": nrm(ks[6], (DEPTH, CONV_A_W, C_A), CONV_A_W ** -0.5),
        "b_dw_a": nrm(ks[7], (DEPTH, C_A), 0.01),
        "ln_a_g": 1.0 + nrm(ks[8], (DEPTH, C_A), 0.02),
        "ln_a_b": nrm(ks[9], (DEPTH, C_A), 0.01),
        "w_a_out": nrm(ks[10], (DEPTH, C_A, D_MODEL), C_A ** -0.5),
        "ln_b_g": 1.0 + nrm(ks[11], (DEPTH, C_B), 0.02),
        "ln_b_b": nrm(ks[12], (DEPTH, C_B), 0.01),
        "w_s": nrm(ks[13], (DEPTH, H_B, CHUNK, CHUNK), CHUNK ** -0.5),
        "b_s": 1.0 + nrm(ks[14], (DEPTH, H_B, CHUNK), 0.01),
        "w_b_out": nrm(ks[15], (DEPTH, C_B, D_MODEL), C_B ** -0.5),
        "w_out": nrm(ks[16], (DEPTH, D_MODEL, D_MODEL), D_MODEL ** -0.5),
        "g_ffn": 1.0 + nrm(ks[17], (DEPTH, D_MODEL), 0.02),
        "w_up": nrm(ks[18], (DEPTH, D_MODEL, 2 * D_FF), D_MODEL ** -0.5),
        "dw_f": nrm(ks[19], (DEPTH, CONV_F_W, D_FF), CONV_F_W ** -0.5),
        "b_dw_f": nrm(ks[20], (DEPTH, D_FF), 0.01),
        "w_down": nrm(ks[21], (DEPTH, D_FF, D_MODEL), D_FF ** -0.5),
        "g_final": 1.0 + nrm(ks[22], (D_MODEL,), 0.02),
    }


def reference(x_prompt, x_sample, state_conv_a, state_ffn_conv, g_mix, w_in, dw_a, b_dw_a,
              ln_a_g, ln_a_b, w_a_out, ln_b_g, ln_b_b, w_s, b_s, w_b_out, w_out,
              g_ffn, w_up, dw_f, b_dw_f, w_down, g_final):
    hp, hs = x_prompt, x_sample
    nb = x_prompt.shape[0]
    conv_p_list, conv_s_list, v_s_list, ffn_p_list, ffn_s_list = [], [], [], [], []
    for l in range(DEPTH):
        mix_w = (w_in[l], dw_a[l], b_dw_a[l], ln_a_g[l], ln_a_b[l], w_a_out[l],
                 ln_b_g[l], ln_b_b[l], w_s[l], b_s[l], w_b_out[l], w_out[l])
        ffn_w = (w_up[l], dw_f[l], b_dw_f[l], w_down[l])
        zero_conv = jnp.zeros((nb, CONV_A_W - 1, C_A), hp.dtype)
        zero_ffn = jnp.zeros((nb, CONV_F_W - 1, D_FF), hp.dtype)
        mp, conv_p, _ = _token_mixers(_rmsnorm(hp, g_mix[l]), zero_conv, *mix_w)
        hp = hp + mp
        fp, ffn_p = _conv_ffn(_rmsnorm(hp, g_ffn[l]), zero_ffn, *ffn_w)
        hp = hp + fp
        ms, conv_s, v_s = _token_mixers(_rmsnorm(hs, g_mix[l]), state_conv_a[l], *mix_w)
        hs = hs + ms
        fs, ffn_s = _conv_ffn(_rmsnorm(hs, g_ffn[l]), state_ffn_conv[l], *ffn_w)
        hs = hs + fs
        conv_p_list.append(conv_p)
        conv_s_list.append(conv_s)
        v_s_list.append(v_s)
        ffn_p_list.append(ffn_p)
        ffn_s_list.append(ffn_s)
    y_prompt = _rmsnorm(hp, g_final)
    y_sample = _rmsnorm(hs, g_final)
    new_conv_a_prompt = jnp.stack(conv_p_list, axis=0)
    new_conv_a_sample = jnp.stack(conv_s_list, axis=0)
    new_chunk_v_sample = jnp.stack(v_s_list, axis=0)
    new_ffn_conv_prompt = jnp.stack(ffn_p_list, axis=0)
    new_ffn_conv_sample = jnp.stack(ffn_s_list, axis=0)
    return (y_prompt, y_sample, new_conv_a_prompt, new_conv_a_sample, new_chunk_v_sample,
            new_ffn_conv_prompt, new_ffn_conv_sample)
```

```python
import numpy as np
import concourse.bass as bass
import concourse.mybir as mybir
from concourse.bass_utils import run_bass_kernel_spmd

F32 = mybir.dt.float32
BF16 = mybir.dt.bfloat16
AF = mybir.ActivationFunctionType
ALU = mybir.AluOpType

D = 1024
DFF = 2816
NIN = 6144
SEQ = 2048
NB_S = 16
TS = 4
CW = 31
EPS = 1e-6
BLK = 64


class Tok:
    __slots__ = ("sem", "val", "clock")

    def __init__(self, sem, val, clock):
        self.sem, self.val, self.clock = sem, val, clock


class Eng:
    def __init__(self, name, sem, is_pe=False):
        self.name, self.sem, self.is_pe = name, sem, is_pe
        self.items = []
        self.count = 0
        self.known = {}


class Sched:
    def __init__(self, nc, sems):
        self.nc = nc
        self.sems = sems
        self.eng = {n: Eng(n, n, n == "pe") for n in ("pe", "act", "dve", "pool", "sp")}
        self.W = {}
        self.R = {}
        self.stream_cnt = {}
        self.cache = {}
        self.pe_pending = False
        self.gs_next = 0
        self.qs_next = 0

    def blocks(self, ap):
        name = ap.tensor.name
        if name == "arena":
            sp = 0
        elif name == "ps":
            sp = 1
        else:
            return ()
        key = (sp, ap.offset, ap.ap, str(ap.dtype))
        r = self.cache.get(key)
        if r is not None:
            return r
        es = 2 if ap.dtype == BF16 else 4
        pat = ap.ap
        pstep = pat[0][0]
        off = ap.offset % pstep if pstep else ap.offset
        dims = [d for d in pat[1:] if d[1] > 1]
        if not dims:
            dims = [(1, 1)]
        outer = 1
        for d in dims[:-1]:
            outer *= d[1]
        res = set()
        if outer > 256:
            lo = off
            hi = off + sum(abs(s) * (c - 1) for s, c in dims) + 1
            res.update(range((lo * es) // BLK, (hi * es - 1) // BLK + 1))
        else:
            def rec(i, base):
                if i == len(dims) - 1:
                    s, c = dims[i]
                    lo = base
                    hi = base + s * (c - 1) + 1
                    res.update(range((lo * es) // BLK, (hi * es - 1) // BLK + 1))
                else:
                    s, c = dims[i]
                    for j in range(c):
                        rec(i + 1, base + j * s)
            rec(0, off)
        if sp == 1:
            res = set(b * BLK // 2048 for b in res)
        r = tuple((sp, b) for b in res)
        self.cache[key] = r
        return r

    def _gather(self, e, outs, ins, ignore=(), extra=()):
        need = {}

        def add(t):
            if t is None:
                return
            cur = need.get(t.sem)
            if cur is None or cur.val < t.val:
                need[t.sem] = t
        for ap in ins:
            for b in self.blocks(ap):
                add(self.W.get(b))
        for ap in outs:
            for b in self.blocks(ap):
                add(self.W.get(b))
                rr = self.R.get(b)
                if rr:
                    for t in rr.values():
                        add(t)
        for t in extra:
            add(t)
        for sem, t in need.items():
            if sem in ignore:
                continue
            if e.is_pe and sem == "pe":
                continue
            val = t.val
            if e.known.get(sem, 0) >= val:
                continue
            e.items.append(("wait", sem, val))
            e.known[sem] = val
            for s2, v2 in t.clock.items():
                if e.known.get(s2, 0) < v2:
                    e.known[s2] = v2

    def _publish(self, tok, outs, ins):
        for ap in ins:
            for b in self.blocks(ap):
                rr = self.R.get(b)
                if rr is None:
                    rr = self.R[b] = {}
                rr[tok.sem] = tok
        for ap in outs:
            for b in self.blocks(ap):
                self.W[b] = tok
                if b in self.R:
                    self.R[b] = {}

    def op(self, en, fn, outs, ins, inc=True, extra=()):
        e = self.eng[en]
        self._gather(e, outs, ins, extra=extra)
        if inc:
            e.count += 1
            val = e.count
            if e.is_pe:
                self.pe_pending = False
        else:
            assert e.is_pe
            val = e.count + 1
            self.pe_pending = True
        tok = Tok(e.sem, val, dict(e.known))
        e.items.append(("op", fn, e.sem if inc else None, 1))
        self._publish(tok, outs, ins)
        return tok

    def dma(self, en, out, in_, stream=None, ignore=(), extra=()):
        e = self.eng[en]
        if stream is None or not stream.startswith("w"):
            if en == "pool":
                stream = "q%d" % (self.qs_next % NQS)
                self.qs_next += 1
            else:
                stream = "g%d" % (self.gs_next % NGS)
                self.gs_next += 1
            prev = self.stream_cnt.get(stream, 0)
            if prev and e.known.get(stream, 0) < 16 * prev:
                e.items.append(("wait", stream, 16 * prev))
                e.known[stream] = 16 * prev
        self._gather(e, [out], [in_], ignore=ignore, extra=extra)
        n = self.stream_cnt.get(stream, 0) + 1
        self.stream_cnt[stream] = n
        tok = Tok(stream, 16 * n, dict(e.known))
        e.items.append(("op", lambda q, o=out, i=in_: q.dma_start(out=o, in_=i), stream, 16))
        self._publish(tok, [out], [in_])
        return tok

    def wait_all_streams(self, en):
        e = self.eng[en]
        for s, n in self.stream_cnt.items():
            if e.known.get(s, 0) < 16 * n:
                e.items.append(("wait", s, 16 * n))
                e.known[s] = 16 * n

    def emit(self, en, q):
        for it in self.eng[en].items:
            if it[0] == "wait":
                q.wait_ge(self.sems[it[1]], it[2])
            else:
                ins = it[1](q)
                if it[2] is not None:
                    ins.then_inc(self.sems[it[2]], it[3])


NGS = 24
NQS = 12
STREAMS = ["wst0", "wst1", "wst2"] + ["wbf%d" % i for i in range(11)] + ["g%d" % i for i in range(NGS)] + ["q%d" % i for i in range(NQS)]
ARENA_BYTES = 206 * 1024


def build_program(debug=False):
    nc = bass.Bass("TRN2", target_bir_lowering=False)

    def din(name, shape):
        return nc.dram_tensor(name, list(shape), F32, kind="ExternalInput").ap()

    def dout(name, shape):
        return nc.dram_tensor(name, list(shape), F32, kind="ExternalOutput").ap()

    x_p = din("x_p", (SEQ, D))
    x_s = din("x_s", (NB_S * TS, D))
    st_a = din("st_a", (NB_S, CW - 1, D))
    st_f = din("st_f", (NB_S * 2, DFF))
    g_mix = din("g_mix", (1, D)); w_in = din("w_in", (24, 128, 2048)); dw_a = din("dw_a", (CW, D))
    b_dw_a = din("b_dw_a", (1, D)); ln_a_g = din("ln_a_g", (1, D)); ln_a_b = din("ln_a_b", (1, D))
    w_a_out = din("w_a_out", (4, 128, 2048)); ln_b_g = din("ln_b_g", (D,)); ln_b_b = din("ln_b_b", (D,))
    w_s = din("w_s", (8, 128, 128)); b_s = din("b_s", (1, 8 * 128)); w_b_out = din("w_b_out", (4, 128, 2048))
    w_out = din("w_out", (4, 128, 2048)); g_ffn = din("g_ffn", (1, D)); w_up = din("w_up", (22, 128, 2048))
    dw_f = din("dw_f", (3, DFF)); b_dw_f = din("b_dw_f", (1, DFF)); w_down = din("w_down", (12, 128, 2048))
    g_final = din("g_final", (D,))

    wscr = nc.dram_tensor("wscr", [70, 128, 2048], BF16, kind="Internal").ap()
    o_yp = dout("o_yp", (SEQ, D)); o_ys = dout("o_ys", (NB_S * TS, D))
    o_cap = dout("o_cap", (CW - 1, D)); o_cas = dout("o_cas", (NB_S, CW - 1, D))
    o_vs = dout("o_vs", (NB_S * TS, D)); o_fp = dout("o_fp", (2, DFF)); o_fs = dout("o_fs", (NB_S * 2, DFF))

    import contextlib
    with contextlib.ExitStack() as es:
        arena = es.enter_context(nc.sbuf_tensor("arena", [128, ARENA_BYTES // 4], F32))
        ps = es.enter_context(nc.psum_tensor("ps", [128, 8 * 512], F32))
        sems = {}
        for n in ["pe", "act", "dve", "pool"] + STREAMS:
            sems[n] = es.enter_context(nc.semaphore("s_" + n))
        block = es.enter_context(nc.Block())
        S = Sched(nc, sems)

        cur = [0]
        named = {}

        def alloc(nbytes, name=None):
            if name is not None and name in named:
                return named[name]
            o = cur[0]
            cur[0] = (o + nbytes + 63) // 64 * 64
            assert cur[0] <= ARENA_BYTES, ("arena overflow", name, cur[0])
            if name is not None:
                named[name] = o
            return o

        def view(off, dt, shape):
            n = 1
            for s in shape[1:]:
                n *= s
            es_ = 2 if dt == BF16 else 4
            a = arena[:, off // 4: off // 4 + (n * es_ + 3) // 4]
            if dt == BF16:
                a = a.bitcast(BF16)
            if len(shape) == 3:
                a = a.rearrange("p (a b) -> p a b", a=shape[1])
            elif len(shape) == 4:
                a = a.rearrange("p (a b c) -> p a b c", a=shape[1], b=shape[2])
            if shape[0] < 128:
                a = a[:shape[0]]
            return a

        def mk(name, dt, shape, alias=None):
            n = 1
            for s in shape[1:]:
                n *= s
            nb = n * (2 if dt == BF16 else 4)
            off = named[alias] if alias else alloc(nb, name)
            return view(off, dt, shape)

        def psb(bank, dt=F32):
            a = ps[:, bank * 512:(bank + 1) * 512]
            return a.bitcast(BF16) if dt == BF16 else a

        h = mk("h", F32, (128, 4, D))
        xs = [mk("xs0", BF16, (128, D)), mk("xs1", BF16, (128, D))]
        xnT = mk("xnT", BF16, (128, 8, 512))
        sg = mk("sg", F32, (128, 2, 512))
        glu = mk("glu", BF16, (128, 8, 544))
        glu32 = mk("glu32", F32, (128, 8, 64))
        ybuf = mk("y", F32, (128, 8, 512))
        u = mk("u", BF16, (128, 8, 512))
        gates = mk("gates", BF16, (128, 16, 512))
        sa = mk("sa", BF16, (128, 8, 512))
        assert named["sa"] == named["gates"] + 16384
        vn = mk("vn", BF16, (128, 4, D))
        vtm = mk("vtm", F32, (128, 4, D))
        ytmp = mk("ytmp", BF16, (128, 4, 512))
        t0 = mk("t0", F32, (128, 2, 512))
        gtmp = mk("gtmp", F32, (128, 2, 512))
        atmp = mk("atmp", F32, (128, 2, 544))
        akeep = mk("akeep", F32, (128, 22, 32))
        diag = mk("diag", BF16, (128, 3, 8, 128))
        ghalo = mk("ghalo", BF16, (128, 8, 30))
        merged = mk("merged", BF16, (128, 8, 512), alias="glu")
        hh = mk("hh", BF16, (128, 22, 512), alias="gates")
        ostage = mk("ostage", F32, (128, DFF), alias="y")
        wst = [mk("wst%d" % i, F32, (128, 2048)) for i in range(3)]
        wbf = [mk("wbf%d" % i, BF16, (128, 2048)) for i in range(5)]
        wbf_late = wbf + [view(named["wst%d" % (q // 2)] + 4096 * (q % 2), BF16, (128, 2048)) for q in range(6)]
        gfin_bc = mk("gfin_bc", F32, (128, D))
        lnbg_bc = mk("lnbg_bc", F32, (128, D))
        lnbb_bc = mk("lnbb_bc", F32, (128, D))
        ident_f = mk("ident_f", F32, (128, 128))
        ident_b = mk("ident_b", BF16, (128, 128))
        mask_ut = mk("mask_ut", F32, (128, 128))
        WsT = mk("WsT", BF16, (128, 8, 128))
        WsTb = mk("WsTb", BF16, (64, 8, 64))
        bsrow = mk("bsrow", BF16, (1, 8, 128))
        bsrowS = mk("bsrowS", BF16, (1, 8, 64))
        ones_row = mk("ones_row", BF16, (1, 128))
        onesN = mk("onesN", BF16, (128, 128))
        colA = mk("colA", F32, (128, 8, 36))
        colF = mk("colF", F32, (128, 22, 4))
        stat = mk("stat", F32, (128, 32))
        nh8 = mk("nh8", F32, (128, 8))
        bnst = mk("bnst", F32, (128, 2, 6))
        mv = mk("mv", F32, (128, 4, 2))

        def T(en, fn, outs, ins, inc=True):
            return S.op(en, fn, outs, ins, inc)

        bank_rr = [0]
        bank_pool = [4]

        def rbank():
            b = bank_rr[0] % bank_pool[0]
            bank_rr[0] += 1
            return b

        def mm(out, lhsT, rhs, start, stop, inc):
            T("pe", lambda q, o=out, l=lhsT, r=rhs, s=start, p=stop: q.matmul(o, lhsT=l, rhs=r, start=s, stop=p),
              [out], [lhsT, rhs], inc=inc)

        def tr(out, in_, ident, inc):
            T("pe", lambda q, o=out, i=in_, d=ident: q.transpose(out=o, in_=i, identity=d), [out], [in_, ident], inc=True)

        def act(out, in_, func, scale=1.0, bias=0.0, accum=None):
            ins = [in_] + [a for a in (scale, bias) if not isinstance(a, float)]
            outs = [out] + ([accum] if accum is not None else [])
            if accum is not None:
                T("act", lambda q: q.activation(out=out, in_=in_, func=func, scale=scale, bias=bias, accum_out=accum), outs, ins)
            else:
                T("act", lambda q: q.activation(out=out, in_=in_, func=func, scale=scale, bias=bias), outs, ins)

        def tt(en, out, in0, in1, op):
            T(en, lambda q: q.tensor_tensor(out=out, in0=in0, in1=in1, op=op), [out], [in0, in1])

        def tsc(en, out, in0, s1, s2, op0, op1=None):
            ins = [in0] + [a for a in (s1, s2) if a is not None and not isinstance(a, float)]
            if op1 is None:
                T(en, lambda q: q.tensor_scalar(out=out, in0=in0, scalar1=s1, scalar2=None, op0=op0), [out], ins)
            else:
                T(en, lambda q: q.tensor_scalar(out=out, in0=in0, scalar1=s1, scalar2=s2, op0=op0, op1=op1), [out], ins)

        def stt(out, in0, scalar, in1, op0, op1):
            ins = [in0, in1] + ([scalar] if not isinstance(scalar, float) else [])
            T("dve", lambda q: q.scalar_tensor_tensor(out=out, in0=in0, scalar=scalar, in1=in1, op0=op0, op1=op1), [out], ins)

        def recip(out, in_):
            T("dve", lambda q: q.reciprocal(out=out, in_=in_), [out], [in_])

        def cp(en, out, in_):
            if en == "act":
                act(out, in_, AF.Identity)
            else:
                T(en, lambda q: q.tensor_copy(out=out, in_=in_), [out], [in_])

        def mset(out, v):
            T("pool", lambda q: q.memset(out, v), [out], [])

        dbg_list = []

        def dbg(name, ap, tl=None):
            if not debug:
                return
            shp = list(ap.shape)
            if ap.dtype == BF16:
                n = 1
                for q_ in shp[1:]:
                    n *= q_
                assert n <= 1024
                sc = vtm[:shp[0], 2, 0:n]
                if len(shp) == 3:
                    sc = sc.rearrange("p (a b) -> p a b", a=shp[1])
                cp("dve", sc, ap)
                ap = sc
            o = nc.dram_tensor("dbg_" + name, shp, F32, kind="ExternalOutput").ap()
            S.dma("pool", o, ap, "out")
            dbg_list.append(name)

        wq = []
        wkind = []
        CONV_ENG = {"gate": "dve", "val": "dve", "v": "act", "u": "act", "g": "dve", "wb": "act", "wa": "act",
                    "wo": "act", "ua": "dve", "ub": "act", "wd": "dve"}
        wstate = {"loaded": 0}

        def wview(i):
            src, nk, ncol = wq[i]
            stv = wst[i % 3][:, 0:nk * ncol]
            bfv = (wbf[i % 5] if i < NCH_TILE else wbf_late[(5 + i - NCH_TILE) % 11])[:, 0:nk * ncol]
            return src, stv, bfv.rearrange("p (k n) -> p k n", k=nk), bfv

        scr_tok = {}

        def wensure(upto):
            upto = min(upto, len(wq) - 1)
            while wstate["loaded"] <= upto:
                i = wstate["loaded"]
                src, stv, _, bff = wview(i)
                nk, ncol = wq[i][1], wq[i][2]
                n32 = nk * ncol // 2
                loc = i % NCH_TILE
                bf32 = view(named["wbf%d" % (i % 5)], F32, (128, 1024))[:, 0:n32]
                if i < NCH_TILE:
                    S.dma("sp", stv, src, "wst%d" % (i % 3))
                    ce = CONV_ENG[wkind[i]]
                    if ce == "alt":
                        ce = "act" if i % 2 else "dve"
                    cp(ce, bff, stv)
                    scr_tok[loc] = S.dma("pool", wscr[loc, :, 0:2 * n32], bff)
                else:
                    S.dma("sp", bff, wscr[loc, :, 0:2 * n32], "wbf%d" % ((5 + i - NCH_TILE) % 11), extra=[scr_tok[loc]])
                wstate["loaded"] += 1

        def wget(i, ahead=3):
            if ahead == 3 and i >= NCH_TILE:
                ahead = 7
            wensure(i + ahead)
            return wview(i)[2]

        def wsrc(w, ci, nk=8, ncol=256):
            return (w[ci, :, 0:nk * ncol], nk, ncol)

        def tile_wseq():
            seq = []
            for i in range(4):
                seq.append(("gate", wsrc(w_in, 4 + i)))
                seq.append(("val", wsrc(w_in, i)))
            for i in range(4):
                seq.append(("v", wsrc(w_in, 12 + i)))
            for i in range(4):
                seq.append(("u", wsrc(w_in, 8 + i)))
            for i in range(8):
                seq.append(("g", wsrc(w_in, 16 + i)))
            for i in range(4):
                seq.append(("wb", wsrc(w_b_out, i)))
                seq.append(("wa", wsrc(w_a_out, i)))
            for i in range(4):
                seq.append(("wo", wsrc(w_out, i)))
            for i in range(11):
                seq.append(("ua", wsrc(w_up, i)))
                seq.append(("ub", wsrc(w_up, 11 + i)))
            for hf in range(2):
                for kg in range(6):
                    nk = 4 if kg < 5 else 2
                    seq.append(("wd", wsrc(w_down, hf * 6 + kg, nk=nk, ncol=512)))
            return seq

        NCH_TILE = 70
        TILES = [("P", i) for i in range(4)] + [("S", 0)]
        wbase = {}
        for tl in TILES:
            wbase[tl] = len(wq)
            for kind, src in tile_wseq():
                wq.append(src)
                wkind.append(kind)

        cctr = [0]

        def cdma(out, in_):
            S.dma("sp", out, in_)

        def setup():
            mset(ident_f, 0.0)
            T("pool", lambda q: q.affine_select(out=ident_f, in_=ident_f, pattern=[[-1, 128]], compare_op=ALU.not_equal,
                                                fill=1.0, base=0, channel_multiplier=1), [ident_f], [ident_f])
            cp("pool", ident_b, ident_f)
            mset(mask_ut, 1.0)
            T("pool", lambda q: q.affine_select(out=mask_ut, in_=mask_ut, pattern=[[1, 128]], compare_op=ALU.is_ge,
                                                fill=0.0, base=0, channel_multiplier=-1), [mask_ut], [mask_ut])
            mset(ones_row, 1.0)
            mset(nh8, -0.5)
            mset(onesN, 1.0 / D)
            mset(WsTb, 0.0)
            mset(akeep, 0.0)
            stg = vtm[:, 0, :]
            cdma(stg[0:31, :], dw_a)
            cdma(stg[31:32, :], b_dw_a)
            cdma(stg[32:33, :], ln_a_g)
            cdma(stg[33:34, :], ln_a_b)
            cdma(stg[34:35, :], g_mix)
            cdma(stg[35:36, :], g_ffn)
            S.dma("sp", h, x_p[0:512, :].rearrange("(c p) d -> p c d", p=128), "x")
            wensure(2)
            b = rbank()
            for j in range(8):
                tr(psb(b)[:, j * 36:(j + 1) * 36], stg[0:36, j * 128:(j + 1) * 128], ident_f[0:36, 0:36], inc=(j == 7))
            cp("dve", colA.rearrange("p a b -> p (a b)"), psb(b)[:, 0:288])
            cdma(gfin_bc, g_final.partition_broadcast(128))
            cdma(lnbg_bc, ln_b_g.partition_broadcast(128))
            cdma(lnbb_bc, ln_b_b.partition_broadcast(128))

        def setup_c():
            stgf = ostage
            cdma(stgf[0:3, :], dw_f)
            cdma(stgf[3:4, :], b_dw_f)
            b = rbank()
            for j in range(22):
                tr(psb(b)[:, j * 4:(j + 1) * 4], stgf[0:4, j * 128:(j + 1) * 128], ident_f[0:4, 0:4], inc=(j == 21))
            cp("dve", colF.rearrange("p a b -> p (a b)"), psb(b)[:, 0:88])
        def setup_b():
            wsn = vtm[:, 1, :].rearrange("p (h s) -> p h s", h=8)
            cdma(wsn, w_s.rearrange("h t s -> t h s"))
            for hh_ in range(8):
                b = rbank()
                tr(psb(b)[:, 0:128], wsn[:, hh_, :], ident_f, inc=True)
                tt("dve", WsT[:, hh_, :], psb(b)[:, 0:128], mask_ut, ALU.mult)
            bsf = vtm[0:1, 2, :]
            cdma(bsf, b_s)
            cp("dve", bsrow.rearrange("p h t -> p (h t)"), bsf)
            bsv = bsf.rearrange("p (h t) -> p h t", h=8)
            for bb in range(NB_S):
                cp("dve", bsrowS[:, :, 4 * bb:4 * bb + 4], bsv[:, :, 0:4])

        def setup_b2():
            btoks = []
            for bb in range(NB_S):
                btoks.append(S.dma("sp", WsTb[4 * bb:4 * bb + 4, :, 4 * bb:4 * bb + 4], WsT[0:4, :, 0:4], "blk",
                                   ignore=set(t.sem for t in btoks)))
            S.op("dve", lambda q: q.tensor_copy(out=WsTb[0:1, 0, 0:2], in_=WsTb[0:1, 0, 0:2]), [WsTb], [], extra=btoks)

        def norm_front(hb, nch, cs):
            for c in range(nch):
                act(xs[c % 2][:cs], hb[:cs, c, :], AF.Square, accum=stat[:cs, c:c + 1])
            act(stat[:cs, 4:4 + nch], stat[:cs, 0:nch], AF.Sqrt, scale=1.0 / D, bias=EPS)
            recip(stat[:cs, 8:8 + nch], stat[:cs, 4:4 + nch])

        def norm_scale(hb, c, cs):
            tsc("dve", xs[c % 2][:cs], hb[:cs, c, :], stat[:cs, 8 + c:9 + c], None, ALU.mult)

        def norm_back(hb, nch, cs, TT, gidx, bank0, prescaled=0):
            for c in range(nch):
                if c >= prescaled:
                    norm_scale(hb, c, cs)
                for k in range(8):
                    o = psb(bank0 + k // 2, BF16)[:, (k % 2) * 512 + c * 128:(k % 2) * 512 + c * 128 + cs]
                    tr(o, xs[c % 2][:cs, k * 128:(k + 1) * 128], ident_b[:cs, :cs], inc=(k == 7))
            for k in range(8):
                i_ = psb(bank0 + k // 2, BF16)[:, (k % 2) * 512:(k % 2) * 512 + TT]
                act(xnT[:, k, 0:TT], i_, AF.Identity, scale=colA[:, k, gidx:gidx + 1])

        def norm_to_T(hb, nch, cs, TT, gidx):
            norm_front(hb, nch, cs)
            norm_back(hb, nch, cs, TT, gidx, 4)

        HV = [h, vtm]
        deferred = []

        def run_tile(tl, idx, nxt):
            kind, ti = tl
            h = HV[idx % 2]
            vtm = HV[1 - idx % 2]
            isS = kind == "S"
            nch, cs, TT = (1, 64, 64) if isS else (4, 128, 512)
            last = (not isS) and ti == 3
            wi = [wbase[tl]]

            def nextw(ahead=3):
                v = wget(wi[0], ahead)
                wi[0] += 1
                return v

            if isS:
                gfull = lambda j: glu[:, j, :].rearrange("p (b t) -> p b t", t=34)
                gnew = lambda j: gfull(j)[:, :, 30:34]
                gtap = lambda j, k: gfull(j)[:, :, k:k + 4]
                as3 = lambda a: a.rearrange("p (b t) -> p b t", t=4)
            else:
                gnew = lambda j: glu[:, j, 30:30 + TT]
                gtap = lambda j, k: glu[:, j, k:k + TT]
                as3 = lambda a: a

            if isS:
                if idx == 0:
                    S.dma("sp", h[:64, 0, :], x_s, "x")
            else:
                pass
                if ti == 0:
                    mset(glu[:, :, 0:30], 0.0)
                    mset(akeep, 0.0)
                else:
                    cp("pool", glu[:, :, 0:30], ghalo)

            def s_history():
                for g4 in range(4):
                    stg = vtm[0:120, g4, :]
                    S.dma("sp", stg, st_a[4 * g4:4 * g4 + 4].rearrange("b j c -> (b j) c"), "hist")
                    for ct in range(8):
                        b = rbank()
                        tr(psb(b)[:, 0:120], stg[:, ct * 128:(ct + 1) * 128], ident_f[0:120, 0:120], inc=True)
                        cp("act" if ct % 2 else "dve", gfull(ct)[:, 4 * g4:4 * g4 + 4, 0:30],
                           psb(b)[:, 0:120].rearrange("p (b j) -> p b j", j=30))
                stg = ostage[0:32, :]
                S.dma("sp", stg, st_f, "hist")
                for m in range(22):
                    b = rbank()
                    tr(psb(b)[:, 0:32], stg[:, m * 128:(m + 1) * 128], ident_f[0:32, 0:32], inc=True)
                    cp("act" if m % 2 else "dve", akeep[:, m, :], psb(b)[:, 0:32])

            if idx == 0:
                norm_to_T(h, nch, cs, TT, 34)

            groups = [] if isS else [(j, g0, min(8, CW - g0)) for j in range(8) for g0 in range(0, CW, 8)]

            def build_group(gi):
                j, g0, ng = groups[gi]
                ds = gi % 3
                tt("dve", diag[:, ds, 0:ng, :], ident_b.unsqueeze(1).broadcast_to([128, ng, 128]),
                   colA[:, j, g0:g0 + ng].unsqueeze(2).broadcast_to([128, ng, 128]), ALU.mult)

            if groups:
                build_group(0)
                build_group(1)

            bank_pool[0] = 8
            for i in range(4):
                wg = nextw()
                wv = nextw()
                for mloc in range(2):
                    j = 2 * i + mloc
                    bg, bv = rbank(), rbank()
                    for k in range(8):
                        mm(psb(bg)[:, 0:TT], wg[:, k, mloc * 128:(mloc + 1) * 128], xnT[:, k, 0:TT], k == 0, k == 7, k == 7)
                    for k in range(8):
                        mm(psb(bv)[:, 0:TT], wv[:, k, mloc * 128:(mloc + 1) * 128], xnT[:, k, 0:TT], k == 0, k == 7, k == 7)
                    act(sg[:, j % 2, 0:TT], psb(bg)[:, 0:TT], AF.Sigmoid)
                    tt("dve", gnew(j), as3(psb(bv)[:, 0:TT]), as3(sg[:, j % 2, 0:TT]), ALU.mult)
                    if isS:
                        tt("dve", glu32[:, j, 0:64], psb(bv)[:, 0:64], sg[:, j % 2, 0:64], ALU.mult)
                    elif last:
                        tt("dve", glu32[:, j, 0:64], psb(bv)[:, 448:512], sg[:, j % 2, 448:512], ALU.mult)

            bank_pool[0] = 4
            if idx == 0:
                setup_c()
            while deferred:
                deferred.pop(0)()
            if isS:
                s_history()
            if isS or last:
                stg = vtm[0:64, 3, :]
                for half in range(2):
                    b = rbank()
                    for jj in range(4):
                        j = half * 4 + jj
                        tr(psb(b)[0:64, jj * 128:(jj + 1) * 128], glu32[:, j, 0:64], ident_f, inc=(jj == 3))
                    cp("act", stg[:, half * 512:(half + 1) * 512], psb(b)[0:64, :])
                if isS:
                    for bb in range(NB_S):
                        S.dma("pool", o_cas[bb, 26:30, :], stg[4 * bb:4 * bb + 4, :], "out")
                    S.dma("pool", o_cas[:, 0:26, :], st_a[:, 4:30, :], "out")
                else:
                    S.dma("pool", o_cap, stg[34:64, :], "out")

            def conv_stats(j):
                slot = j % 2
                act(ytmp[:, 2 + slot, 0:TT], ybuf[:, j, 0:TT], AF.Square)
                cp("pool", ytmp[:, slot, 0:TT], ybuf[:, j, 0:TT])
                mm(psb(6)[:, 0:TT], onesN, ytmp[:, slot, 0:TT], j == 0, j == 7, True)
                mm(psb(7)[:, 0:TT], onesN, ytmp[:, 2 + slot, 0:TT], j == 0, j == 7, True)

            dcount = [0]
            s_conv = []
            if isS:
                def mk_conv(j, k):
                    def f():
                        yv = as3(ybuf[:, j, 0:TT])
                        if k == 0:
                            tsc("dve", yv, gtap(j, 0), colA[:, j, 0:1], colA[:, j, 31:32], ALU.mult, ALU.add)
                        else:
                            stt(yv, gtap(j, k), colA[:, j, k:k + 1], yv, ALU.mult, ALU.add)
                    return f
                for k in range(CW):
                    for j in range(8):
                        s_conv.append(mk_conv(j, k))

            def pump(n):
                for _ in range(min(n, len(s_conv))):
                    s_conv.pop(0)()

            conv_st = {"gi": 0, "cb": None}

            def conv_ctile(j):
                for g0 in range(0, CW, 8):
                    gi = conv_st["gi"]
                    jj, g0_, ng = groups[gi]
                    assert jj == j and g0_ == g0
                    if g0 == 0:
                        conv_st["cb"] = rbank()
                    cb = conv_st["cb"]
                    if gi + 2 < len(groups):
                        build_group(gi + 2)
                    for kk in range(ng):
                        k = g0 + kk
                        mm(as3(psb(cb)[:, 0:TT]), diag[:, gi % 3, kk, :], gtap(j, k), k == 0, k == CW - 1, kk == ng - 1)
                    if g0 + ng == CW:
                        if j >= 1:
                            conv_stats(j - 1)
                        act(ybuf[:, j, 0:TT], psb(cb)[:, 0:TT], AF.Identity, bias=colA[:, j, 31:32])
                    conv_st["gi"] += 1

            interleave = (idx == 0) and not isS
            if not isS and not interleave:
                for j in range(8):
                    conv_ctile(j)
                conv_stats(7)
            if idx == 0:
                setup_b()
            if idx == 1:
                setup_b2()
            if not isS and not last:
                cp("pool", ghalo, glu[:, :, 512:542])
            if isS:
                dbg("y", ybuf[:, :, 0:64])
                dbg("xnT", xnT[:, :, 0:64])

            def ln_a_dve():
                mean_sb, rstd_sb = t0[:, 0, 0:TT], t0[:, 1, 0:TT]
                tmpv = gtmp[:, 0, 0:TT]
                act(mean_sb, psb(6)[:, 0:TT], AF.Identity)
                tt("dve", tmpv, mean_sb, mean_sb, ALU.mult)
                tt("dve", tmpv, psb(7)[:, 0:TT], tmpv, ALU.subtract)
                act(tmpv, tmpv, AF.Sqrt, bias=EPS)
                recip(rstd_sb, tmpv)
                for j in range(8):
                    en_ = "pool" if j >= 4 else "dve"
                    tt(en_, ybuf[:, j, 0:TT], ybuf[:, j, 0:TT], mean_sb, ALU.subtract)
                    tt(en_, ybuf[:, j, 0:TT], ybuf[:, j, 0:TT], rstd_sb, ALU.mult)


            if not isS and not interleave:
                ln_a_dve()

            vb = [4, 5, 0, 1]
            for pair in range(2):
                ws = [nextw(), nextw()]
                for c in range(nch):
                    bk = vb[c] if pair == 0 else [2, 3, 4, 5][c]
                    for q in range(2):
                        for k in range(8):
                            mm(psb(bk)[:cs, q * 256:(q + 1) * 256], xnT[:, k, c * cs:(c + 1) * cs], ws[q][:, k, :],
                               k == 0, k == 7, k == 7)
                    act(vtm[:cs, c, pair * 512:(pair + 1) * 512], psb(bk)[:cs, :], AF.Gelu)
                pump(32)
            for c in range(nch):
                for q in range(2):
                    T("dve", lambda e, c=c, q=q: e.bn_stats(out=bnst[:cs, q, :], in_=vtm[:cs, c, q * 512:(q + 1) * 512]),
                      [bnst[:cs, q, :]], [vtm[:cs, c, q * 512:(q + 1) * 512]])
                T("dve", lambda e, c=c: e.bn_aggr(out=mv[:cs, c, :], in_=bnst[:cs].rearrange("p a b -> p (a b)")), [mv[:cs, c, :]], [bnst[:cs]])
            tsc("dve", stat[:cs, 12:12 + nch], mv[:cs, 0:nch, 1], EPS, None, ALU.add)
            tt("pool", stat[:cs, 16:16 + nch], stat[:cs, 12:12 + nch], nh8[:cs, 0:nch], ALU.pow)
            def ln_b_apply():
                for c in range(nch):
                    stt(vtm[:cs, c, :], vtm[:cs, c, :], mv[:cs, c, 0:1], lnbg_bc[:cs], ALU.subtract, ALU.mult)
                    if isS:
                        stt(vtm[:cs, 1, :], vtm[:cs, c, :], stat[:cs, 16 + c:17 + c], lnbb_bc[:cs], ALU.mult, ALU.add)
                        cp("dve", vn[:cs, c, :], vtm[:cs, 1, :])
                        S.dma("pool", o_vs, vtm[:cs, 1, :], "out")
                    else:
                        stt(vn[:cs, c, :], vtm[:cs, c, :], stat[:cs, 16 + c:17 + c], lnbb_bc[:cs], ALU.mult, ALU.add)

            bank_pool[0] = 6 if interleave else 8
            for i in range(4):
                w = nextw()
                for mloc in range(2):
                    j = 2 * i + mloc
                    b = rbank()
                    for k in range(8):
                        mm(psb(b)[:, 0:TT], w[:, k, mloc * 128:(mloc + 1) * 128], xnT[:, k, 0:TT], k == 0, k == 7, k == 7)
                    act(u[:, j, 0:TT], psb(b)[:, 0:TT], AF.Gelu)
                pump(16)
                if interleave:
                    conv_ctile(i)
            for j in (range(0) if (isS or interleave) else range(8)):
                act(sa[:, j, 0:TT], ybuf[:, j, 0:TT], AF.Silu, scale=colA[:, j, 32:33], bias=colA[:, j, 33:34])
            for i in range(8):
                w = nextw()
                for mloc in range(2):
                    j = 2 * i + mloc
                    b = rbank()
                    for k in range(8):
                        mm(psb(b)[:, 0:TT], w[:, k, mloc * 128:(mloc + 1) * 128], xnT[:, k, 0:TT], k == 0, k == 7, k == 7)
                    act(gates[:, j, 0:TT], psb(b)[:, 0:TT], AF.Sigmoid)
                pump(16)
                if interleave and i < 4:
                    conv_ctile(4 + i)
                if interleave and i == 3:
                    conv_stats(7)
                    ln_a_dve()
                    for j_ in range(8):
                        act(sa[:, j_, 0:TT], ybuf[:, j_, 0:TT], AF.Silu, scale=colA[:, j_, 32:33], bias=colA[:, j_, 33:34])
            if isS:
                pump(len(s_conv))
                for j in range(8):
                    conv_stats(j)
                ln_a_dve()
                for j in range(8):
                    act(sa[:, j, 0:TT], ybuf[:, j, 0:TT], AF.Silu, scale=colA[:, j, 32:33], bias=colA[:, j, 33:34])

            if isS:
                dbg("sa", sa[:, :, 0:64])
                dbg("u", u[:, :, 0:64])
                dbg("gates", gates[:, :, 0:64])
                dbg("vn", vn[:64, 0, :])
            ln_b_apply()
            bank_pool[0] = 8
            for hd in range(8):
                b = rbank()
                for c in range(nch):
                    o = psb(b)[:, c * 128:c * 128 + cs]
                    if isS:
                        mm(o, vn[:cs, c, hd * 128:(hd + 1) * 128], WsTb[:, hd, :], True, False, False)
                        mm(o, ones_row, bsrowS[:, hd, :], False, True, True)
                    else:
                        mm(o, vn[:cs, c, hd * 128:(hd + 1) * 128], WsT[:, hd, :], True, False, False)
                        mm(o, ones_row, bsrow[:, hd, :], False, True, True)
                tt("dve", u[:, hd, 0:TT], u[:, hd, 0:TT], psb(b)[:, 0:TT], ALU.mult)

            if isS:
                dbg("prod", u[:, :, 0:64])
            ta, tb = sg[:, 0, 0:TT], sg[:, 1, 0:TT]
            for i in range(4):
                wb_ = nextw()
                wa_ = nextw()
                for mloc in range(2):
                    j = 2 * i + mloc
                    bb_, ba_ = rbank(), rbank()
                    for k in range(8):
                        mm(psb(bb_)[:, 0:TT], wb_[:, k, mloc * 128:(mloc + 1) * 128], u[:, k, 0:TT], k == 0, k == 7, k == 7)
                    for k in range(8):
                        mm(psb(ba_)[:, 0:TT], wa_[:, k, mloc * 128:(mloc + 1) * 128], sa[:, k, 0:TT], k == 0, k == 7, k == 7)
                    tt("dve", tb, psb(bb_)[:, 0:TT], gates[:, 8 + j, 0:TT], ALU.mult)
                    tt("dve", ta, psb(ba_)[:, 0:TT], gates[:, j, 0:TT], ALU.mult)
                    tt("dve", merged[:, j, 0:TT], ta, tb, ALU.add)

            bank_pool[0] = 4
            wos = [nextw(0), nextw(0), nextw(0), nextw(1)]

            def n2_transposes(c):
                for k in range(8):
                    o = psb(4 + k // 2, BF16)[:, (k % 2) * 512 + c * 128:(k % 2) * 512 + c * 128 + cs]
                    tr(o, xs[c % 2][:cs, k * 128:(k + 1) * 128], ident_b[:cs, :cs], inc=(k == 7))

            for c in range(nch):
                for i in range(4):
                    b = rbank()
                    for k in range(8):
                        mm(psb(b)[:cs, 0:256], merged[:, k, c * cs:(c + 1) * cs], wos[i][:, k, :], k == 0, k == 7, k == 7)
                    tt("dve", h[:cs, c, i * 256:(i + 1) * 256], psb(b)[:cs, 0:256], h[:cs, c, i * 256:(i + 1) * 256], ALU.add)
                act(xs[c % 2][:cs], h[:cs, c, :], AF.Square, accum=stat[:cs, c:c + 1])
                act(stat[:cs, 4 + c:5 + c], stat[:cs, c:c + 1], AF.Sqrt, scale=1.0 / D, bias=EPS)
                recip(stat[:cs, 8 + c:9 + c], stat[:cs, 4 + c:5 + c])
                tsc("dve", xs[c % 2][:cs], h[:cs, c, :], stat[:cs, 8 + c:9 + c], None, ALU.mult)
                if c >= 1:
                    n2_transposes(c - 1)
            n2_transposes(nch - 1)
            for k in range(8):
                i_ = psb(4 + k // 2, BF16)[:, (k % 2) * 512:(k % 2) * 512 + TT]
                act(xnT[:, k, 0:TT], i_, AF.Identity, scale=colA[:, k, 35:36])
            bank_pool[0] = 8

            if isS:
                dbg("merged", merged[:, :, 0:64])
                dbg("h1", h[:64, 0, :])

            if isS:
                a3 = lambda s: atmp[:, s, 0:96].rearrange("p (b t) -> p b t", t=6)
                anew = lambda s: a3(s)[:, :, 2:6]
                atap = lambda s, k: a3(s)[:, :, k:k + 4]
                ahalo = lambda s: a3(s)[:, :, 0:2]
                akv = lambda m: akeep[:, m, :].rearrange("p (b t) -> p b t", t=2)
                alast = lambda s: a3(s)[:, :, 4:6]
            else:
                anew = lambda s: atmp[:, s, 2:2 + TT]
                atap = lambda s, k: atmp[:, s, k:k + TT]
                ahalo = lambda s: atmp[:, s, 0:2]
                akv = lambda m: akeep[:, m, 0:2]
                alast = lambda s: atmp[:, s, TT:TT + 2]
            if isS:
                apl = lambda b_: as3(psb(b_)[:, 0:TT])[:, :, 2:4]
            else:
                apl = lambda b_: psb(b_)[:, TT - 2:TT]

            def ffn_stage2(m, s, bb_):
                tv = t0[:, s, 0:TT]
                stt(as3(tv), atap(s, 1), colF[:, m, 1:2], as3(tv), ALU.mult, ALU.add)
                stt(as3(tv), atap(s, 0), colF[:, m, 0:1], as3(tv), ALU.mult, ALU.add)
                act(gtmp[:, s, 0:TT], tv, AF.Gelu)
                tt("dve", hh[:, m, 0:TT], gtmp[:, s, 0:TT], psb(bb_)[:, 0:TT], ALU.mult)

            if nxt is not None:
                tn = nxt[1]
                if nxt[0] == "S":
                    S.dma("sp", vtm[:64, 0, :], x_s, "x")
                else:
                    S.dma("sp", vtm, x_p[tn * 512:(tn + 1) * 512, :].rearrange("(c p) d -> p c d", p=128), "x")
            n_nch, n_cs = ((1, 64) if (nxt is not None and nxt[0] == "S") else (4, 128))
            pend = None
            for i in range(11):
                wa_ = nextw()
                wb_ = nextw()
                for mloc in range(2):
                    m = 2 * i + mloc
                    s = m % 2
                    ba_, bb_ = (2 * m) % 8, (2 * m + 1) % 8
                    for k in range(8):
                        mm(psb(ba_)[:, 0:TT], wa_[:, k, mloc * 128:(mloc + 1) * 128], xnT[:, k, 0:TT], k == 0, k == 7, k == 7)
                    for k in range(8):
                        mm(psb(bb_)[:, 0:TT], wb_[:, k, mloc * 128:(mloc + 1) * 128], xnT[:, k, 0:TT], k == 0, k == 7, k == 7)
                    cp("dve", ahalo(s), akv(m))
                    act(anew(s), as3(psb(ba_)[:, 0:TT]), AF.Identity)
                    act(akv(m), apl(ba_), AF.Identity)
                    act(as3(t0[:, s, 0:TT]), as3(psb(ba_)[:, 0:TT]), AF.Identity, scale=colF[:, m, 2:3], bias=colF[:, m, 3:4])
                    if pend is not None:
                        ffn_stage2(*pend)
                    pend = (m, s, bb_)
            ffn_stage2(*pend)

            if isS:
                dbg("colA", colA)
                dbg("colF", colF)
                dbg("stat", stat)
                dbg("xn2T", xnT[:, :, 0:64])
                dbg("hh", hh[:, 0:16, 0:64])
                dbg("akeep", akeep)
            if isS or last:
                nr = 32 if isS else 2
                for g in range(6):
                    b = rbank()
                    nm = 4 if g < 5 else 2
                    for mm_ in range(nm):
                        m = 4 * g + mm_
                        tr(psb(b)[0:nr, mm_ * 128:(mm_ + 1) * 128], akeep[:, m, 0:nr], ident_f, inc=(mm_ == nm - 1))
                    cp("act", ostage[0:nr, g * 512:g * 512 + nm * 128], psb(b)[0:nr, 0:nm * 128])
                S.dma("pool", o_fs if isS else o_fp, ostage[0:nr, :], "out")

            bank_pool[0] = 4
            if nxt is not None:
                norm_front(vtm, n_nch, n_cs)
                for c_ in range(min(2, n_nch)):
                    norm_scale(vtm, c_, n_cs)
            for hf in range(2):
                wb0 = 4 if hf == 0 else 0
                for kg in range(6):
                    w = nextw()
                    nk = 4 if kg < 5 else 2
                    for c in range(nch):
                        for kk in range(nk):
                            k = 4 * kg + kk
                            mm(psb(wb0 + c)[:cs, :], hh[:, k, c * cs:(c + 1) * cs], w[:, kk, :], k == 0, k == 21,
                               (kk == nk - 1))
                    if hf == 1 and kg == 2 and nxt is not None:
                        norm_back(vtm, n_nch, n_cs, n_nch * n_cs, 34, 4, prescaled=min(2, n_nch))
                for c in range(nch):
                    tt("dve", h[:cs, c, hf * 512:(hf + 1) * 512], psb(wb0 + c)[:cs, :], h[:cs, c, hf * 512:(hf + 1) * 512], ALU.add)

            if isS:
                dbg("h2", h[:64, 0, :])
            def tail():
                for c in range(nch):
                    act(xs[c % 2][:cs], h[:cs, c, :], AF.Square, accum=stat[:cs, 20 + c:21 + c])
                act(stat[:cs, 24:24 + nch], stat[:cs, 20:20 + nch], AF.Sqrt, scale=1.0 / D, bias=EPS)
                recip(stat[:cs, 28:28 + nch], stat[:cs, 24:24 + nch])
                for c in range(nch):
                    stt(h[:cs, c, :], h[:cs, c, :], stat[:cs, 28 + c:29 + c], gfin_bc[:cs], ALU.mult, ALU.mult)
                if isS:
                    S.dma("pool", o_ys, h[:64, 0, :], "out")
                else:
                    S.dma("pool", o_yp[ti * 512:(ti + 1) * 512, :].rearrange("(c p) d -> p c d", p=128), h, "out")


            if nxt is None:
                tail()
            else:
                deferred.append(tail)

        setup()
        for idx, tl in enumerate(TILES):
            run_tile(tl, idx, TILES[idx + 1] if idx + 1 < len(TILES) else None)
        S.wait_all_streams("pool")

        @block.sync
        def _(q):
            S.emit("sp", q)

        @block.gpsimd
        def _(q):
            S.emit("pool", q)

        @block.scalar
        def _(q):
            S.emit("act", q)

        @block.vector
        def _(q):
            S.emit("dve", q)

        @block.tensor
        def _(q):
            S.emit("pe", q)

    return nc


_NC = None


def _chunk_cols(w, ncol=256):
    K, N = w.shape
    nk, nq = K // 128, N // ncol
    return np.ascontiguousarray(w.reshape(nk, 128, nq, ncol).transpose(2, 1, 0, 3).reshape(nq, 128, nk * ncol))


def _chunk_wdown(w):
    wp = np.zeros((3072, D), np.float32)
    wp[:DFF] = w
    a = wp.reshape(6, 4, 128, 2, 512).transpose(3, 0, 2, 1, 4)
    return np.ascontiguousarray(a.reshape(12, 128, 2048))


def kernel(**inp):
    global _NC
    f = lambda a: np.ascontiguousarray(np.asarray(a, dtype=np.float32))
    if _NC is None:
        _NC = build_program()
    nc = _NC
    shared = {
        "g_mix": f(inp["g_mix"]).reshape(1, D), "w_in": _chunk_cols(f(inp["w_in"])[0]), "dw_a": f(inp["dw_a"])[0],
        "b_dw_a": f(inp["b_dw_a"]).reshape(1, D), "ln_a_g": f(inp["ln_a_g"]).reshape(1, D),
        "ln_a_b": f(inp["ln_a_b"]).reshape(1, D), "w_a_out": _chunk_cols(f(inp["w_a_out"])[0]),
        "ln_b_g": f(inp["ln_b_g"])[0], "ln_b_b": f(inp["ln_b_b"])[0], "w_s": f(inp["w_s"])[0],
        "b_s": f(inp["b_s"]).reshape(1, 8 * 128), "w_b_out": _chunk_cols(f(inp["w_b_out"])[0]), "w_out": _chunk_cols(f(inp["w_out"])[0]),
        "g_ffn": f(inp["g_ffn"]).reshape(1, D), "w_up": _chunk_cols(f(inp["w_up"])[0]), "dw_f": f(inp["dw_f"])[0],
        "b_dw_f": f(inp["b_dw_f"]).reshape(1, DFF), "w_down": _chunk_wdown(f(inp["w_down"])[0]), "g_final": f(inp["g_final"]),
    }
    xp, xsm = f(inp["x_prompt"]), f(inp["x_sample"])
    sa_, sf_ = f(inp["state_conv_a"])[0], f(inp["state_ffn_conv"])[0]
    in_maps = []
    for c in range(8):
        m = dict(shared)
        m["x_p"] = xp[c]
        m["x_s"] = xsm[16 * c:16 * c + 16].reshape(64, D)
        m["st_a"] = sa_[16 * c:16 * c + 16]
        m["st_f"] = sf_[16 * c:16 * c + 16].reshape(32, DFF)
        in_maps.append(m)
    res = run_bass_kernel_spmd(nc, in_maps, core_ids=list(range(8)))
    R = res.results
    y_p = np.stack([R[c]["o_yp"] for c in range(8)], 0)
    y_s = np.concatenate([R[c]["o_ys"].reshape(16, 4, D) for c in range(8)], 0)
    cap = np.stack([R[c]["o_cap"] for c in range(8)], 0)[None]
    cas = np.concatenate([R[c]["o_cas"] for c in range(8)], 0)[None]
    vs = np.concatenate([R[c]["o_vs"].reshape(16, 4, D) for c in range(8)], 0)[None]
    fp = np.stack([R[c]["o_fp"] for c in range(8)], 0)[None]
    fs = np.concatenate([R[c]["o_fs"].reshape(16, 2, DFF) for c in range(8)], 0)[None]
    return (y_p.astype(np.float32), y_s.astype(np.float32), cap.astype(np.float32), cas.astype(np.float32),
            vs.astype(np.float32), fp.astype(np.float32), fs.astype(np.float32))
```

```python
import numpy as np
import concourse.bass as bass
import concourse.mybir as mybir
from concourse.bass_utils import run_bass_kernel_spmd

F32 = mybir.dt.float32
BF16 = mybir.dt.bfloat16
AF = mybir.ActivationFunctionType
ALU = mybir.AluOpType

D = 1024
DFF = 2816
NIN = 6144
SEQ = 2048
NB_S = 16
TS = 4
CW = 31
EPS = 1e-6
BLK = 64


class Tok:
    __slots__ = ("sem", "val", "clock")

    def __init__(self, sem, val, clock):
        self.sem, self.val, self.clock = sem, val, clock


class Eng:
    def __init__(self, name, sem, is_pe=False):
        self.name, self.sem, self.is_pe = name, sem, is_pe
        self.items = []
        self.count = 0
        self.known = {}


class Sched:
    def __init__(self, nc, sems):
        self.nc = nc
        self.sems = sems
        self.eng = {n: Eng(n, n, n == "pe") for n in ("pe", "act", "dve", "pool", "sp")}
        self.W = {}
        self.R = {}
        self.stream_cnt = {}
        self.cache = {}
        self.pe_pending = False
        self.gs_next = 0
        self.qs_next = 0

    def blocks(self, ap):
        name = ap.tensor.name
        if name == "arena":
            sp = 0
        elif name == "ps":
            sp = 1
        else:
            return ()
        key = (sp, ap.offset, ap.ap, str(ap.dtype))
        r = self.cache.get(key)
        if r is not None:
            return r
        es = 2 if ap.dtype == BF16 else 4
        pat = ap.ap
        pstep = pat[0][0]
        off = ap.offset % pstep if pstep else ap.offset
        dims = [d for d in pat[1:] if d[1] > 1]
        if not dims:
            dims = [(1, 1)]
        outer = 1
        for d in dims[:-1]:
            outer *= d[1]
        res = set()
        if outer > 256:
            lo = off
            hi = off + sum(abs(s) * (c - 1) for s, c in dims) + 1
            res.update(range((lo * es) // BLK, (hi * es - 1) // BLK + 1))
        else:
            def rec(i, base):
                if i == len(dims) - 1:
                    s, c = dims[i]
                    lo = base
                    hi = base + s * (c - 1) + 1
                    res.update(range((lo * es) // BLK, (hi * es - 1) // BLK + 1))
                else:
                    s, c = dims[i]
                    for j in range(c):
                        rec(i + 1, base + j * s)
            rec(0, off)
        if sp == 1:
            res = set(b * BLK // 2048 for b in res)
        r = tuple((sp, b) for b in res)
        self.cache[key] = r
        return r

    def _gather(self, e, outs, ins, ignore=(), extra=()):
        need = {}

        def add(t):
            if t is None:
                return
            cur = need.get(t.sem)
            if cur is None or cur.val < t.val:
                need[t.sem] = t
        for ap in ins:
            for b in self.blocks(ap):
                add(self.W.get(b))
        for ap in outs:
            for b in self.blocks(ap):
                add(self.W.get(b))
                rr = self.R.get(b)
                if rr:
                    for t in rr.values():
                        add(t)
        for t in extra:
            add(t)
        for sem, t in need.items():
            if sem in ignore:
                continue
            if e.is_pe and sem == "pe":
                continue
            val = t.val
            if e.known.get(sem, 0) >= val:
                continue
            e.items.append(("wait", sem, val))
            e.known[sem] = val
            for s2, v2 in t.clock.items():
                if e.known.get(s2, 0) < v2:
                    e.known[s2] = v2

    def _publish(self, tok, outs, ins):
        for ap in ins:
            for b in self.blocks(ap):
                rr = self.R.get(b)
                if rr is None:
                    rr = self.R[b] = {}
                rr[tok.sem] = tok
        for ap in outs:
            for b in self.blocks(ap):
                self.W[b] = tok
                if b in self.R:
                    self.R[b] = {}

    def op(self, en, fn, outs, ins, inc=True, extra=()):
        e = self.eng[en]
        self._gather(e, outs, ins, extra=extra)
        if inc:
            e.count += 1
            val = e.count
            if e.is_pe:
                self.pe_pending = False
        else:
            assert e.is_pe
            val = e.count + 1
            self.pe_pending = True
        tok = Tok(e.sem, val, dict(e.known))
        e.items.append(("op", fn, e.sem if inc else None, 1))
        self._publish(tok, outs, ins)
        return tok

    def dma(self, en, out, in_, stream=None, ignore=(), extra=()):
        e = self.eng[en]
        if stream is None or not stream.startswith("w"):
            if en == "pool":
                stream = "q%d" % (self.qs_next % NQS)
                self.qs_next += 1
            else:
                stream = "g%d" % (self.gs_next % NGS)
                self.gs_next += 1
            prev = self.stream_cnt.get(stream, 0)
            if prev and e.known.get(stream, 0) < 16 * prev:
                e.items.append(("wait", stream, 16 * prev))
                e.known[stream] = 16 * prev
        self._gather(e, [out], [in_], ignore=ignore, extra=extra)
        n = self.stream_cnt.get(stream, 0) + 1
        self.stream_cnt[stream] = n
        tok = Tok(stream, 16 * n, dict(e.known))
        e.items.append(("op", lambda q, o=out, i=in_: q.dma_start(out=o, in_=i), stream, 16))
        self._publish(tok, [out], [in_])
        return tok

    def wait_all_streams(self, en):
        e = self.eng[en]
        for s, n in self.stream_cnt.items():
            if e.known.get(s, 0) < 16 * n:
                e.items.append(("wait", s, 16 * n))
                e.known[s] = 16 * n

    def emit(self, en, q):
        for it in self.eng[en].items:
            if it[0] == "wait":
                q.wait_ge(self.sems[it[1]], it[2])
            else:
                ins = it[1](q)
                if it[2] is not None:
                    ins.then_inc(self.sems[it[2]], it[3])


NGS = 24
NQS = 12
STREAMS = ["wst0", "wst1", "wst2"] + ["wbf%d" % i for i in range(11)] + ["g%d" % i for i in range(NGS)] + ["q%d" % i for i in range(NQS)]
ARENA_BYTES = 206 * 1024


def build_program(debug=False):
    nc = bass.Bass("TRN2", target_bir_lowering=False)

    def din(name, shape):
        return nc.dram_tensor(name, list(shape), F32, kind="ExternalInput").ap()

    def dout(name, shape):
        return nc.dram_tensor(name, list(shape), F32, kind="ExternalOutput").ap()

    x_p = din("x_p", (SEQ, D))
    x_s = din("x_s", (NB_S * TS, D))
    st_a = din("st_a", (NB_S, CW - 1, D))
    st_f = din("st_f", (NB_S * 2, DFF))
    g_mix = din("g_mix", (1, D)); w_in = din("w_in", (24, 128, 2048)); dw_a = din("dw_a", (CW, D))
    b_dw_a = din("b_dw_a", (1, D)); ln_a_g = din("ln_a_g", (1, D)); ln_a_b = din("ln_a_b", (1, D))
    w_a_out = din("w_a_out", (4, 128, 2048)); ln_b_g = din("ln_b_g", (D,)); ln_b_b = din("ln_b_b", (D,))
    w_s = din("w_s", (8, 128, 128)); b_s = din("b_s", (1, 8 * 128)); w_b_out = din("w_b_out", (4, 128, 2048))
    w_out = din("w_out", (4, 128, 2048)); g_ffn = din("g_ffn", (1, D)); w_up = din("w_up", (22, 128, 2048))
    dw_f = din("dw_f", (3, DFF)); b_dw_f = din("b_dw_f", (1, DFF)); w_down = din("w_down", (12, 128, 2048))
    g_final = din("g_final", (D,))

    wscr = nc.dram_tensor("wscr", [70, 128, 2048], BF16, kind="Internal").ap()
    o_yp = dout("o_yp", (SEQ, D)); o_ys = dout("o_ys", (NB_S * TS, D))
    o_cap = dout("o_cap", (CW - 1, D)); o_cas = dout("o_cas", (NB_S, CW - 1, D))
    o_vs = dout("o_vs", (NB_S * TS, D)); o_fp = dout("o_fp", (2, DFF)); o_fs = dout("o_fs", (NB_S * 2, DFF))

    import contextlib
    with contextlib.ExitStack() as es:
        arena = es.enter_context(nc.sbuf_tensor("arena", [128, ARENA_BYTES // 4], F32))
        ps = es.enter_context(nc.psum_tensor("ps", [128, 8 * 512], F32))
        sems = {}
        for n in ["pe", "act", "dve", "pool"] + STREAMS:
            sems[n] = es.enter_context(nc.semaphore("s_" + n))
        block = es.enter_context(nc.Block())
        S = Sched(nc, sems)

        cur = [0]
        named = {}

        def alloc(nbytes, name=None):
            if name is not None and name in named:
                return named[name]
            o = cur[0]
            cur[0] = (o + nbytes + 63) // 64 * 64
            assert cur[0] <= ARENA_BYTES, ("arena overflow", name, cur[0])
            if name is not None:
                named[name] = o
            return o

        def view(off, dt, shape):
            n = 1
            for s in shape[1:]:
                n *= s
            es_ = 2 if dt == BF16 else 4
            a = arena[:, off // 4: off // 4 + (n * es_ + 3) // 4]
            if dt == BF16:
                a = a.bitcast(BF16)
            if len(shape) == 3:
                a = a.rearrange("p (a b) -> p a b", a=shape[1])
            elif len(shape) == 4:
                a = a.rearrange("p (a b c) -> p a b c", a=shape[1], b=shape[2])
            if shape[0] < 128:
                a = a[:shape[0]]
            return a

        def mk(name, dt, shape, alias=None):
            n = 1
            for s in shape[1:]:
                n *= s
            nb = n * (2 if dt == BF16 else 4)
            off = named[alias] if alias else alloc(nb, name)
            return view(off, dt, shape)

        def psb(bank, dt=F32):
            a = ps[:, bank * 512:(bank + 1) * 512]
            return a.bitcast(BF16) if dt == BF16 else a

        h = mk("h", F32, (128, 4, D))
        xs = [mk("xs0", BF16, (128, D)), mk("xs1", BF16, (128, D))]
        xnT = mk("xnT", BF16, (128, 8, 512))
        sg = mk("sg", F32, (128, 2, 512))
        glu = mk("glu", BF16, (128, 8, 544))
        glu32 = mk("glu32", F32, (128, 8, 64))
        ybuf = mk("y", F32, (128, 8, 512))
        u = mk("u", BF16, (128, 8, 512))
        gates = mk("gates", BF16, (128, 16, 512))
        sa = mk("sa", BF16, (128, 8, 512))
        assert named["sa"] == named["gates"] + 16384
        vn = mk("vn", BF16, (128, 4, D))
        vtm = mk("vtm", F32, (128, 4, D))
        ytmp = mk("ytmp", BF16, (128, 4, 512))
        t0 = mk("t0", F32, (128, 2, 512))
        gtmp = mk("gtmp", F32, (128, 2, 512))
        atmp = mk("atmp", F32, (128, 2, 544))
        akeep = mk("akeep", F32, (128, 22, 32))
        diag = mk("diag", BF16, (128, 3, 8, 128))
        ghalo = mk("ghalo", BF16, (128, 8, 30))
        merged = mk("merged", BF16, (128, 8, 512), alias="glu")
        hh = mk("hh", BF16, (128, 22, 512), alias="gates")
        ostage = mk("ostage", F32, (128, DFF), alias="y")
        wst = [mk("wst%d" % i, F32, (128, 2048)) for i in range(3)]
        wbf = [mk("wbf%d" % i, BF16, (128, 2048)) for i in range(5)]
        wbf_late = wbf + [view(named["wst%d" % (q // 2)] + 4096 * (q % 2), BF16, (128, 2048)) for q in range(6)]
        gfin_bc = mk("gfin_bc", F32, (128, D))
        lnbg_bc = mk("lnbg_bc", F32, (128, D))
        lnbb_bc = mk("lnbb_bc", F32, (128, D))
        ident_f = mk("ident_f", F32, (128, 128))
        ident_b = mk("ident_b", BF16, (128, 128))
        mask_ut = mk("mask_ut", F32, (128, 128))
        WsT = mk("WsT", BF16, (128, 8, 128))
        WsTb = mk("WsTb", BF16, (64, 8, 64))
        bsrow = mk("bsrow", BF16, (1, 8, 128))
        bsrowS = mk("bsrowS", BF16, (1, 8, 64))
        ones_row = mk("ones_row", BF16, (1, 128))
        onesN = mk("onesN", BF16, (128, 128))
        colA = mk("colA", F32, (128, 8, 36))
        colF = mk("colF", F32, (128, 22, 4))
        stat = mk("stat", F32, (128, 32))
        nh8 = mk("nh8", F32, (128, 8))
        bnst = mk("bnst", F32, (128, 2, 6))
        mv = mk("mv", F32, (128, 4, 2))

        def T(en, fn, outs, ins, inc=True):
            return S.op(en, fn, outs, ins, inc)

        bank_rr = [0]
        bank_pool = [4]

        def rbank():
            b = bank_rr[0] % bank_pool[0]
            bank_rr[0] += 1
            return b

        def mm(out, lhsT, rhs, start, stop, inc):
            T("pe", lambda q, o=out, l=lhsT, r=rhs, s=start, p=stop: q.matmul(o, lhsT=l, rhs=r, start=s, stop=p),
              [out], [lhsT, rhs], inc=inc)

        def tr(out, in_, ident, inc):
            T("pe", lambda q, o=out, i=in_, d=ident: q.transpose(out=o, in_=i, identity=d), [out], [in_, ident], inc=True)

        def act(out, in_, func, scale=1.0, bias=0.0, accum=None):
            ins = [in_] + [a for a in (scale, bias) if not isinstance(a, float)]
            outs = [out] + ([accum] if accum is not None else [])
            if accum is not None:
                T("act", lambda q: q.activation(out=out, in_=in_, func=func, scale=scale, bias=bias, accum_out=accum), outs, ins)
            else:
                T("act", lambda q: q.activation(out=out, in_=in_, func=func, scale=scale, bias=bias), outs, ins)

        def tt(en, out, in0, in1, op):
            T(en, lambda q: q.tensor_tensor(out=out, in0=in0, in1=in1, op=op), [out], [in0, in1])

        def tsc(en, out, in0, s1, s2, op0, op1=None):
            ins = [in0] + [a for a in (s1, s2) if a is not None and not isinstance(a, float)]
            if op1 is None:
                T(en, lambda q: q.tensor_scalar(out=out, in0=in0, scalar1=s1, scalar2=None, op0=op0), [out], ins)
            else:
                T(en, lambda q: q.tensor_scalar(out=out, in0=in0, scalar1=s1, scalar2=s2, op0=op0, op1=op1), [out], ins)

        def stt(out, in0, scalar, in1, op0, op1):
            ins = [in0, in1] + ([scalar] if not isinstance(scalar, float) else [])
            T("dve", lambda q: q.scalar_tensor_tensor(out=out, in0=in0, scalar=scalar, in1=in1, op0=op0, op1=op1), [out], ins)

        def recip(out, in_):
            T("dve", lambda q: q.reciprocal(out=out, in_=in_), [out], [in_])

        def cp(en, out, in_):
            if en == "act":
                act(out, in_, AF.Identity)
            else:
                T(en, lambda q: q.tensor_copy(out=out, in_=in_), [out], [in_])

        def mset(out, v):
            T("pool", lambda q: q.memset(out, v), [out], [])

        dbg_list = []

        def dbg(name, ap, tl=None):
            if not debug:
                return
            shp = list(ap.shape)
            if ap.dtype == BF16:
                n = 1
                for q_ in shp[1:]:
                    n *= q_
                assert n <= 1024
                sc = vtm[:shp[0], 2, 0:n]
                if len(shp) == 3:
                    sc = sc.rearrange("p (a b) -> p a b", a=shp[1])
                cp("dve", sc, ap)
                ap = sc
            o = nc.dram_tensor("dbg_" + name, shp, F32, kind="ExternalOutput").ap()
            S.dma("pool", o, ap, "out")
            dbg_list.append(name)

        wq = []
        wkind = []
        CONV_ENG = {"gate": "dve", "val": "dve", "v": "act", "u": "act", "g": "dve", "wb": "act", "wa": "act",
                    "wo": "act", "ua": "dve", "ub": "act", "wd": "dve"}
        wstate = {"loaded": 0}

        def wview(i):
            src, nk, ncol = wq[i]
            stv = wst[i % 3][:, 0:nk * ncol]
            bfv = (wbf[i % 5] if i < NCH_TILE else wbf_late[(5 + i - NCH_TILE) % 11])[:, 0:nk * ncol]
            return src, stv, bfv.rearrange("p (k n) -> p k n", k=nk), bfv

        scr_tok = {}

        def wensure(upto):
            upto = min(upto, len(wq) - 1)
            while wstate["loaded"] <= upto:
                i = wstate["loaded"]
                src, stv, _, bff = wview(i)
                nk, ncol = wq[i][1], wq[i][2]
                n32 = nk * ncol // 2
                loc = i % NCH_TILE
                bf32 = view(named["wbf%d" % (i % 5)], F32, (128, 1024))[:, 0:n32]
                if i < NCH_TILE:
                    S.dma("sp", stv, src, "wst%d" % (i % 3))
                    ce = CONV_ENG[wkind[i]]
                    if ce == "alt":
                        ce = "act" if i % 2 else "dve"
                    cp(ce, bff, stv)
                    scr_tok[loc] = S.dma("pool", wscr[loc, :, 0:2 * n32], bff)
                else:
                    S.dma("sp", bff, wscr[loc, :, 0:2 * n32], "wbf%d" % ((5 + i - NCH_TILE) % 11), extra=[scr_tok[loc]])
                wstate["loaded"] += 1

        def wget(i, ahead=3):
            if ahead == 3 and i >= NCH_TILE:
                ahead = 7
            wensure(i + ahead)
            return wview(i)[2]

        def wsrc(w, ci, nk=8, ncol=256):
            return (w[ci, :, 0:nk * ncol], nk, ncol)

        def tile_wseq():
            seq = []
            for i in range(4):
                seq.append(("gate", wsrc(w_in, 4 + i)))
                seq.append(("val", wsrc(w_in, i)))
            for i in range(4):
                seq.append(("v", wsrc(w_in, 12 + i)))
            for i in range(4):
                seq.append(("u", wsrc(w_in, 8 + i)))
            for i in range(8):
                seq.append(("g", wsrc(w_in, 16 + i)))
            for i in range(4):
                seq.append(("wb", wsrc(w_b_out, i)))
                seq.append(("wa", wsrc(w_a_out, i)))
            for i in range(4):
                seq.append(("wo", wsrc(w_out, i)))
            for i in range(11):
                seq.append(("ua", wsrc(w_up, i)))
                seq.append(("ub", wsrc(w_up, 11 + i)))
            for hf in range(2):
                for kg in range(6):
                    nk = 4 if kg < 5 else 2
                    seq.append(("wd", wsrc(w_down, hf * 6 + kg, nk=nk, ncol=512)))
            return seq

        NCH_TILE = 70
        TILES = [("P", i) for i in range(4)] + [("S", 0)]
        wbase = {}
        for tl in TILES:
            wbase[tl] = len(wq)
            for kind, src in tile_wseq():
                wq.append(src)
                wkind.append(kind)

        cctr = [0]

        def cdma(out, in_):
            S.dma("sp", out, in_)

        def setup():
            mset(ident_f, 0.0)
            T("pool", lambda q: q.affine_select(out=ident_f, in_=ident_f, pattern=[[-1, 128]], compare_op=ALU.not_equal,
                                                fill=1.0, base=0, channel_multiplier=1), [ident_f], [ident_f])
            cp("pool", ident_b, ident_f)
            mset(mask_ut, 1.0)
            T("pool", lambda q: q.affine_select(out=mask_ut, in_=mask_ut, pattern=[[1, 128]], compare_op=ALU.is_ge,
                                                fill=0.0, base=0, channel_multiplier=-1), [mask_ut], [mask_ut])
            mset(ones_row, 1.0)
            mset(nh8, -0.5)
            mset(onesN, 1.0 / D)
            mset(WsTb, 0.0)
            mset(akeep, 0.0)
            stg = vtm[:, 0, :]
            cdma(stg[0:31, :], dw_a)
            cdma(stg[31:32, :], b_dw_a)
            cdma(stg[32:33, :], ln_a_g)
            cdma(stg[33:34, :], ln_a_b)
            cdma(stg[34:35, :], g_mix)
            cdma(stg[35:36, :], g_ffn)
            S.dma("sp", h, x_p[0:512, :].rearrange("(c p) d -> p c d", p=128), "x")
            wensure(2)
            b = rbank()
            for j in range(8):
                tr(psb(b)[:, j * 36:(j + 1) * 36], stg[0:36, j * 128:(j + 1) * 128], ident_f[0:36, 0:36], inc=(j == 7))
            cp("dve", colA.rearrange("p a b -> p (a b)"), psb(b)[:, 0:288])
            cdma(gfin_bc, g_final.partition_broadcast(128))
            cdma(lnbg_bc, ln_b_g.partition_broadcast(128))
            cdma(lnbb_bc, ln_b_b.partition_broadcast(128))

        def setup_c():
            stgf = ostage
            cdma(stgf[0:3, :], dw_f)
            cdma(stgf[3:4, :], b_dw_f)
            b = rbank()
            for j in range(22):
                tr(psb(b)[:, j * 4:(j + 1) * 4], stgf[0:4, j * 128:(j + 1) * 128], ident_f[0:4, 0:4], inc=(j == 21))
            cp("dve", colF.rearrange("p a b -> p (a b)"), psb(b)[:, 0:88])
        def setup_b():
            wsn = vtm[:, 1, :].rearrange("p (h s) -> p h s", h=8)
            cdma(wsn, w_s.rearrange("h t s -> t h s"))
            for hh_ in range(8):
                b = rbank()
                tr(psb(b)[:, 0:128], wsn[:, hh_, :], ident_f, inc=True)
                tt("dve", WsT[:, hh_, :], psb(b)[:, 0:128], mask_ut, ALU.mult)
            bsf = vtm[0:1, 2, :]
            cdma(bsf, b_s)
            cp("dve", bsrow.rearrange("p h t -> p (h t)"), bsf)
            bsv = bsf.rearrange("p (h t) -> p h t", h=8)
            for bb in range(NB_S):
                cp("dve", bsrowS[:, :, 4 * bb:4 * bb + 4], bsv[:, :, 0:4])

        def setup_b2():
            btoks = []
            for bb in range(NB_S):
                btoks.append(S.dma("sp", WsTb[4 * bb:4 * bb + 4, :, 4 * bb:4 * bb + 4], WsT[0:4, :, 0:4], "blk",
                                   ignore=set(t.sem for t in btoks)))
            S.op("dve", lambda q: q.tensor_copy(out=WsTb[0:1, 0, 0:2], in_=WsTb[0:1, 0, 0:2]), [WsTb], [], extra=btoks)

        def norm_front(hb, nch, cs):
            for c in range(nch):
                act(xs[c % 2][:cs], hb[:cs, c, :], AF.Square, accum=stat[:cs, c:c + 1])
            act(stat[:cs, 4:4 + nch], stat[:cs, 0:nch], AF.Sqrt, scale=1.0 / D, bias=EPS)
            recip(stat[:cs, 8:8 + nch], stat[:cs, 4:4 + nch])

        def norm_scale(hb, c, cs):
            tsc("dve", xs[c % 2][:cs], hb[:cs, c, :], stat[:cs, 8 + c:9 + c], None, ALU.mult)

        def norm_back(hb, nch, cs, TT, gidx, bank0, prescaled=0):
            for c in range(nch):
                if c >= prescaled:
                    norm_scale(hb, c, cs)
                for k in range(8):
                    o = psb(bank0 + k // 2, BF16)[:, (k % 2) * 512 + c * 128:(k % 2) * 512 + c * 128 + cs]
                    tr(o, xs[c % 2][:cs, k * 128:(k + 1) * 128], ident_b[:cs, :cs], inc=(k == 7))
            for k in range(8):
                i_ = psb(bank0 + k // 2, BF16)[:, (k % 2) * 512:(k % 2) * 512 + TT]
                act(xnT[:, k, 0:TT], i_, AF.Identity, scale=colA[:, k, gidx:gidx + 1])

        def norm_to_T(hb, nch, cs, TT, gidx):
            norm_front(hb, nch, cs)
            norm_back(hb, nch, cs, TT, gidx, 4)

        HV = [h, vtm]
        deferred = []

        def run_tile(tl, idx, nxt):
            kind, ti = tl
            h = HV[idx % 2]
            vtm = HV[1 - idx % 2]
            isS = kind == "S"
            nch, cs, TT = (1, 64, 64) if isS else (4, 128, 512)
            last = (not isS) and ti == 3
            wi = [wbase[tl]]

            def nextw(ahead=3):
                v = wget(wi[0], ahead)
                wi[0] += 1
                return v

            if isS:
                gfull = lambda j: glu[:, j, :].rearrange("p (b t) -> p b t", t=34)
                gnew = lambda j: gfull(j)[:, :, 30:34]
                gtap = lambda j, k: gfull(j)[:, :, k:k + 4]
                as3 = lambda a: a.rearrange("p (b t) -> p b t", t=4)
            else:
                gnew = lambda j: glu[:, j, 30:30 + TT]
                gtap = lambda j, k: glu[:, j, k:k + TT]
                as3 = lambda a: a

            if isS:
                if idx == 0:
                    S.dma("sp", h[:64, 0, :], x_s, "x")
            else:
                pass
                if ti == 0:
                    mset(glu[:, :, 0:30], 0.0)
                    mset(akeep, 0.0)
                else:
                    cp("pool", glu[:, :, 0:30], ghalo)

            def s_history():
                for g4 in range(4):
                    stg = vtm[0:120, g4, :]
                    S.dma("sp", stg, st_a[4 * g4:4 * g4 + 4].rearrange("b j c -> (b j) c"), "hist")
                    for ct in range(8):
                        b = rbank()
                        tr(psb(b)[:, 0:120], stg[:, ct * 128:(ct + 1) * 128], ident_f[0:120, 0:120], inc=True)
                        cp("act" if ct % 2 else "dve", gfull(ct)[:, 4 * g4:4 * g4 + 4, 0:30],
                           psb(b)[:, 0:120].rearrange("p (b j) -> p b j", j=30))
                stg = ostage[0:32, :]
                S.dma("sp", stg, st_f, "hist")
                for m in range(22):
                    b = rbank()
                    tr(psb(b)[:, 0:32], stg[:, m * 128:(m + 1) * 128], ident_f[0:32, 0:32], inc=True)
                    cp("act" if m % 2 else "dve", akeep[:, m, :], psb(b)[:, 0:32])

            if idx == 0:
                norm_to_T(h, nch, cs, TT, 34)

            groups = [] if isS else [(j, g0, min(8, CW - g0)) for j in range(8) for g0 in range(0, CW, 8)]

            def build_group(gi):
                j, g0, ng = groups[gi]
                ds = gi % 3
                tt("dve", diag[:, ds, 0:ng, :], ident_b.unsqueeze(1).broadcast_to([128, ng, 128]),
                   colA[:, j, g0:g0 + ng].unsqueeze(2).broadcast_to([128, ng, 128]), ALU.mult)

            if groups:
                build_group(0)
                build_group(1)

            bank_pool[0] = 8
            for i in range(4):
                wg = nextw()
                wv = nextw()
                for mloc in range(2):
                    j = 2 * i + mloc
                    bg, bv = rbank(), rbank()
                    for k in range(8):
                        mm(psb(bg)[:, 0:TT], wg[:, k, mloc * 128:(mloc + 1) * 128], xnT[:, k, 0:TT], k == 0, k == 7, k == 7)
                    for k in range(8):
                        mm(psb(bv)[:, 0:TT], wv[:, k, mloc * 128:(mloc + 1) * 128], xnT[:, k, 0:TT], k == 0, k == 7, k == 7)
                    act(sg[:, j % 2, 0:TT], psb(bg)[:, 0:TT], AF.Sigmoid)
                    tt("dve", gnew(j), as3(psb(bv)[:, 0:TT]), as3(sg[:, j % 2, 0:TT]), ALU.mult)
                    if isS:
                        tt("dve", glu32[:, j, 0:64], psb(bv)[:, 0:64], sg[:, j % 2, 0:64], ALU.mult)
                    elif last:
                        tt("dve", glu32[:, j, 0:64], psb(bv)[:, 448:512], sg[:, j % 2, 448:512], ALU.mult)

            bank_pool[0] = 4
            if idx == 0:
                setup_c()
            while deferred:
                deferred.pop(0)()
            if isS:
                s_history()
            if isS or last:
                stg = vtm[0:64, 3, :]
                for half in range(2):
                    b = rbank()
                    for jj in range(4):
                        j = half * 4 + jj
                        tr(psb(b)[0:64, jj * 128:(jj + 1) * 128], glu32[:, j, 0:64], ident_f, inc=(jj == 3))
                    cp("act", stg[:, half * 512:(half + 1) * 512], psb(b)[0:64, :])
                if isS:
                    for bb in range(NB_S):
                        S.dma("pool", o_cas[bb, 26:30, :], stg[4 * bb:4 * bb + 4, :], "out")
                    S.dma("pool", o_cas[:, 0:26, :], st_a[:, 4:30, :], "out")
                else:
                    S.dma("pool", o_cap, stg[34:64, :], "out")

            def conv_stats(j):
                slot = j % 2
                act(ytmp[:, 2 + slot, 0:TT], ybuf[:, j, 0:TT], AF.Square)
                cp("pool", ytmp[:, slot, 0:TT], ybuf[:, j, 0:TT])
                mm(psb(6)[:, 0:TT], onesN, ytmp[:, slot, 0:TT], j == 0, j == 7, True)
                mm(psb(7)[:, 0:TT], onesN, ytmp[:, 2 + slot, 0:TT], j == 0, j == 7, True)

            dcount = [0]
            s_conv = []
            if isS:
                def mk_conv(j, k):
                    def f():
                        yv = as3(ybuf[:, j, 0:TT])
                        if k == 0:
                            tsc("dve", yv, gtap(j, 0), colA[:, j, 0:1], colA[:, j, 31:32], ALU.mult, ALU.add)
                        else:
                            stt(yv, gtap(j, k), colA[:, j, k:k + 1], yv, ALU.mult, ALU.add)
                    return f
                for k in range(CW):
                    for j in range(8):
                        s_conv.append(mk_conv(j, k))

            def pump(n):
                for _ in range(min(n, len(s_conv))):
                    s_conv.pop(0)()

            conv_st = {"gi": 0, "cb": None}

            def conv_ctile(j):
                for g0 in range(0, CW, 8):
                    gi = conv_st["gi"]
                    jj, g0_, ng = groups[gi]
                    assert jj == j and g0_ == g0
                    if g0 == 0:
                        conv_st["cb"] = rbank()
                    cb = conv_st["cb"]
                    if gi + 2 < len(groups):
                        build_group(gi + 2)
                    for kk in range(ng):
                        k = g0 + kk
                        mm(as3(psb(cb)[:, 0:TT]), diag[:, gi % 3, kk, :], gtap(j, k), k == 0, k == CW - 1, kk == ng - 1)
                    if g0 + ng == CW:
                        if j >= 1:
                            conv_stats(j - 1)
                        act(ybuf[:, j, 0:TT], psb(cb)[:, 0:TT], AF.Identity, bias=colA[:, j, 31:32])
                    conv_st["gi"] += 1

            interleave = (idx == 0) and not isS
            if not isS and not interleave:
                for j in range(8):
                    conv_ctile(j)
                conv_stats(7)
            if idx == 0:
                setup_b()
            if idx == 1:
                setup_b2()
            if not isS and not last:
                cp("pool", ghalo, glu[:, :, 512:542])
            if isS:
                dbg("y", ybuf[:, :, 0:64])
                dbg("xnT", xnT[:, :, 0:64])

            def ln_a_dve():
                mean_sb, rstd_sb = t0[:, 0, 0:TT], t0[:, 1, 0:TT]
                tmpv = gtmp[:, 0, 0:TT]
                act(mean_sb, psb(6)[:, 0:TT], AF.Identity)
                tt("dve", tmpv, mean_sb, mean_sb, ALU.mult)
                tt("dve", tmpv, psb(7)[:, 0:TT], tmpv, ALU.subtract)
                act(tmpv, tmpv, AF.Sqrt, bias=EPS)
                recip(rstd_sb, tmpv)
                for j in range(8):
                    en_ = "pool" if j >= 4 else "dve"
                    tt(en_, ybuf[:, j, 0:TT], ybuf[:, j, 0:TT], mean_sb, ALU.subtract)
                    tt(en_, ybuf[:, j, 0:TT], ybuf[:, j, 0:TT], rstd_sb, ALU.mult)


            if not isS and not interleave:
                ln_a_dve()

            vb = [4, 5, 0, 1]
            for pair in range(2):
                ws = [nextw(), nextw()]
                for c in range(nch):
                    bk = vb[c] if pair == 0 else [2, 3, 4, 5][c]
                    for q in range(2):
                        for k in range(8):
                            mm(psb(bk)[:cs, q * 256:(q + 1) * 256], xnT[:, k, c * cs:(c + 1) * cs], ws[q][:, k, :],
                               k == 0, k == 7, k == 7)
                    act(vtm[:cs, c, pair * 512:(pair + 1) * 512], psb(bk)[:cs, :], AF.Gelu)
                pump(32)
            for c in range(nch):
                for q in range(2):
                    T("dve", lambda e, c=c, q=q: e.bn_stats(out=bnst[:cs, q, :], in_=vtm[:cs, c, q * 512:(q + 1) * 512]),
                      [bnst[:cs, q, :]], [vtm[:cs, c, q * 512:(q + 1) * 512]])
                T("dve", lambda e, c=c: e.bn_aggr(out=mv[:cs, c, :], in_=bnst[:cs].rearrange("p a b -> p (a b)")), [mv[:cs, c, :]], [bnst[:cs]])
            tsc("dve", stat[:cs, 12:12 + nch], mv[:cs, 0:nch, 1], EPS, None, ALU.add)
            tt("pool", stat[:cs, 16:16 + nch], stat[:cs, 12:12 + nch], nh8[:cs, 0:nch], ALU.pow)
            def ln_b_apply():
                for c in range(nch):
                    stt(vtm[:cs, c, :], vtm[:cs, c, :], mv[:cs, c, 0:1], lnbg_bc[:cs], ALU.subtract, ALU.mult)
                    if isS:
                        stt(vtm[:cs, 1, :], vtm[:cs, c, :], stat[:cs, 16 + c:17 + c], lnbb_bc[:cs], ALU.mult, ALU.add)
                        cp("dve", vn[:cs, c, :], vtm[:cs, 1, :])
                        S.dma("pool", o_vs, vtm[:cs, 1, :], "out")
                    else:
                        stt(vn[:cs, c, :], vtm[:cs, c, :], stat[:cs, 16 + c:17 + c], lnbb_bc[:cs], ALU.mult, ALU.add)

            bank_pool[0] = 6 if interleave else 8
            for i in range(4):
                w = nextw()
                for mloc in range(2):
                    j = 2 * i + mloc
                    b = rbank()
                    for k in range(8):
                        mm(psb(b)[:, 0:TT], w[:, k, mloc * 128:(mloc + 1) * 128], xnT[:, k, 0:TT], k == 0, k == 7, k == 7)
                    act(u[:, j, 0:TT], psb(b)[:, 0:TT], AF.Gelu)
                pump(16)
                if interleave:
                    conv_ctile(i)
            for j in (range(0) if (isS or interleave) else range(8)):
                act(sa[:, j, 0:TT], ybuf[:, j, 0:TT], AF.Silu, scale=colA[:, j, 32:33], bias=colA[:, j, 33:34])
            for i in range(8):
                w = nextw()
                for mloc in range(2):
                    j = 2 * i + mloc
                    b = rbank()
                    for k in range(8):
                        mm(psb(b)[:, 0:TT], w[:, k, mloc * 128:(mloc + 1) * 128], xnT[:, k, 0:TT], k == 0, k == 7, k == 7)
                    act(gates[:, j, 0:TT], psb(b)[:, 0:TT], AF.Sigmoid)
                pump(16)
                if interleave and i < 4:
                    conv_ctile(4 + i)
                if interleave and i == 3:
                    conv_stats(7)
                    ln_a_dve()
            if interleave:
                for j_ in range(8):
                    act(sa[:, j_, 0:TT], ybuf[:, j_, 0:TT], AF.Silu, scale=colA[:, j_, 32:33], bias=colA[:, j_, 33:34])
            if isS:
                pump(len(s_conv))
                for j in range(8):
                    conv_stats(j)
                ln_a_dve()
                for j in range(8):
                    act(sa[:, j, 0:TT], ybuf[:, j, 0:TT], AF.Silu, scale=colA[:, j, 32:33], bias=colA[:, j, 33:34])

            if isS:
                dbg("sa", sa[:, :, 0:64])
                dbg("u", u[:, :, 0:64])
                dbg("gates", gates[:, :, 0:64])
                dbg("vn", vn[:64, 0, :])
            ln_b_apply()
            bank_pool[0] = 8
            for hd in range(8):
                b = rbank()
                for c in range(nch):
                    o = psb(b)[:, c * 128:c * 128 + cs]
                    if isS:
                        mm(o, vn[:cs, c, hd * 128:(hd + 1) * 128], WsTb[:, hd, :], True, False, False)
                        mm(o, ones_row, bsrowS[:, hd, :], False, True, True)
                    else:
                        mm(o, vn[:cs, c, hd * 128:(hd + 1) * 128], WsT[:, hd, :], True, False, False)
                        mm(o, ones_row, bsrow[:, hd, :], False, True, True)
                tt("dve", u[:, hd, 0:TT], u[:, hd, 0:TT], psb(b)[:, 0:TT], ALU.mult)

            if isS:
                dbg("prod", u[:, :, 0:64])
            ta, tb = sg[:, 0, 0:TT], sg[:, 1, 0:TT]
            for i in range(4):
                wb_ = nextw()
                wa_ = nextw()
                for mloc in range(2):
                    j = 2 * i + mloc
                    bb_, ba_ = rbank(), rbank()
                    for k in range(8):
                        mm(psb(bb_)[:, 0:TT], wb_[:, k, mloc * 128:(mloc + 1) * 128], u[:, k, 0:TT], k == 0, k == 7, k == 7)
                    for k in range(8):
                        mm(psb(ba_)[:, 0:TT], wa_[:, k, mloc * 128:(mloc + 1) * 128], sa[:, k, 0:TT], k == 0, k == 7, k == 7)
                    tt("dve", tb, psb(bb_)[:, 0:TT], gates[:, 8 + j, 0:TT], ALU.mult)
                    tt("dve", ta, psb(ba_)[:, 0:TT], gates[:, j, 0:TT], ALU.mult)
                    tt("dve", merged[:, j, 0:TT], ta, tb, ALU.add)

            bank_pool[0] = 4
            wos = [nextw(0), nextw(0), nextw(0), nextw(1)]

            def n2_transposes(c):
                for k in range(8):
                    o = psb(4 + k // 2, BF16)[:, (k % 2) * 512 + c * 128:(k % 2) * 512 + c * 128 + cs]
                    tr(o, xs[c % 2][:cs, k * 128:(k + 1) * 128], ident_b[:cs, :cs], inc=(k == 7))

            for c in range(nch):
                for i in range(4):
                    b = rbank()
                    for k in range(8):
                        mm(psb(b)[:cs, 0:256], merged[:, k, c * cs:(c + 1) * cs], wos[i][:, k, :], k == 0, k == 7, k == 7)
                    tt("dve", h[:cs, c, i * 256:(i + 1) * 256], psb(b)[:cs, 0:256], h[:cs, c, i * 256:(i + 1) * 256], ALU.add)
                act(xs[c % 2][:cs], h[:cs, c, :], AF.Square, accum=stat[:cs, c:c + 1])
                act(stat[:cs, 4 + c:5 + c], stat[:cs, c:c + 1], AF.Sqrt, scale=1.0 / D, bias=EPS)
                recip(stat[:cs, 8 + c:9 + c], stat[:cs, 4 + c:5 + c])
                tsc("dve", xs[c % 2][:cs], h[:cs, c, :], stat[:cs, 8 + c:9 + c], None, ALU.mult)
                if c >= 1:
                    n2_transposes(c - 1)
            n2_transposes(nch - 1)
            for k in range(8):
                i_ = psb(4 + k // 2, BF16)[:, (k % 2) * 512:(k % 2) * 512 + TT]
                act(xnT[:, k, 0:TT], i_, AF.Identity, scale=colA[:, k, 35:36])
            bank_pool[0] = 8

            if isS:
                dbg("merged", merged[:, :, 0:64])
                dbg("h1", h[:64, 0, :])

            if isS:
                a3 = lambda s: atmp[:, s, 0:96].rearrange("p (b t) -> p b t", t=6)
                anew = lambda s: a3(s)[:, :, 2:6]
                atap = lambda s, k: a3(s)[:, :, k:k + 4]
                ahalo = lambda s: a3(s)[:, :, 0:2]
                akv = lambda m: akeep[:, m, :].rearrange("p (b t) -> p b t", t=2)
                alast = lambda s: a3(s)[:, :, 4:6]
            else:
                anew = lambda s: atmp[:, s, 2:2 + TT]
                atap = lambda s, k: atmp[:, s, k:k + TT]
                ahalo = lambda s: atmp[:, s, 0:2]
                akv = lambda m: akeep[:, m, 0:2]
                alast = lambda s: atmp[:, s, TT:TT + 2]
            if isS:
                apl = lambda b_: as3(psb(b_)[:, 0:TT])[:, :, 2:4]
            else:
                apl = lambda b_: psb(b_)[:, TT - 2:TT]

            def ffn_stage2(m, s, bb_):
                tv = t0[:, s, 0:TT]
                stt(as3(tv), atap(s, 1), colF[:, m, 1:2], as3(tv), ALU.mult, ALU.add)
                stt(as3(tv), atap(s, 0), colF[:, m, 0:1], as3(tv), ALU.mult, ALU.add)
                act(gtmp[:, s, 0:TT], tv, AF.Gelu)
                tt("dve", hh[:, m, 0:TT], gtmp[:, s, 0:TT], psb(bb_)[:, 0:TT], ALU.mult)

            if nxt is not None:
                tn = nxt[1]
                if nxt[0] == "S":
                    S.dma("sp", vtm[:64, 0, :], x_s, "x")
                else:
                    S.dma("sp", vtm, x_p[tn * 512:(tn + 1) * 512, :].rearrange("(c p) d -> p c d", p=128), "x")
            n_nch, n_cs = ((1, 64) if (nxt is not None and nxt[0] == "S") else (4, 128))
            pend = None
            for i in range(11):
                wa_ = nextw()
                wb_ = nextw()
                for mloc in range(2):
                    m = 2 * i + mloc
                    s = m % 2
                    ba_, bb_ = (2 * m) % 8, (2 * m + 1) % 8
                    for k in range(8):
                        mm(psb(ba_)[:, 0:TT], wa_[:, k, mloc * 128:(mloc + 1) * 128], xnT[:, k, 0:TT], k == 0, k == 7, k == 7)
                    for k in range(8):
                        mm(psb(bb_)[:, 0:TT], wb_[:, k, mloc * 128:(mloc + 1) * 128], xnT[:, k, 0:TT], k == 0, k == 7, k == 7)
                    cp("dve", ahalo(s), akv(m))
                    act(anew(s), as3(psb(ba_)[:, 0:TT]), AF.Identity)
                    act(akv(m), apl(ba_), AF.Identity)
                    act(as3(t0[:, s, 0:TT]), as3(psb(ba_)[:, 0:TT]), AF.Identity, scale=colF[:, m, 2:3], bias=colF[:, m, 3:4])
                    if pend is not None:
                        ffn_stage2(*pend)
                    pend = (m, s, bb_)
            ffn_stage2(*pend)

            if isS:
                dbg("colA", colA)
                dbg("colF", colF)
                dbg("stat", stat)
                dbg("xn2T", xnT[:, :, 0:64])
                dbg("hh", hh[:, 0:16, 0:64])
                dbg("akeep", akeep)
            if isS or last:
                nr = 32 if isS else 2
                for g in range(6):
                    b = rbank()
                    nm = 4 if g < 5 else 2
                    for mm_ in range(nm):
                        m = 4 * g + mm_
                        tr(psb(b)[0:nr, mm_ * 128:(mm_ + 1) * 128], akeep[:, m, 0:nr], ident_f, inc=(mm_ == nm - 1))
                    cp("act", ostage[0:nr, g * 512:g * 512 + nm * 128], psb(b)[0:nr, 0:nm * 128])
                S.dma("pool", o_fs if isS else o_fp, ostage[0:nr, :], "out")

            bank_pool[0] = 4
            if nxt is not None:
                norm_front(vtm, n_nch, n_cs)
                for c_ in range(min(2, n_nch)):
                    norm_scale(vtm, c_, n_cs)
            for hf in range(2):
                wb0 = 4 if hf == 0 else 0
                for kg in range(6):
                    w = nextw()
                    nk = 4 if kg < 5 else 2
                    for c in range(nch):
                        for kk in range(nk):
                            k = 4 * kg + kk
                            mm(psb(wb0 + c)[:cs, :], hh[:, k, c * cs:(c + 1) * cs], w[:, kk, :], k == 0, k == 21,
                               (kk == nk - 1))
                    if hf == 1 and kg == 2 and nxt is not None:
                        norm_back(vtm, n_nch, n_cs, n_nch * n_cs, 34, 4, prescaled=min(2, n_nch))
                for c in range(nch):
                    tt("dve", h[:cs, c, hf * 512:(hf + 1) * 512], psb(wb0 + c)[:cs, :], h[:cs, c, hf * 512:(hf + 1) * 512], ALU.add)

            if isS:
                dbg("h2", h[:64, 0, :])
            def tail():
                for c in range(nch):
                    act(xs[c % 2][:cs], h[:cs, c, :], AF.Square, accum=stat[:cs, 20 + c:21 + c])
                act(stat[:cs, 24:24 + nch], stat[:cs, 20:20 + nch], AF.Sqrt, scale=1.0 / D, bias=EPS)
                recip(stat[:cs, 28:28 + nch], stat[:cs, 24:24 + nch])
                for c in range(nch):
                    stt(h[:cs, c, :], h[:cs, c, :], stat[:cs, 28 + c:29 + c], gfin_bc[:cs], ALU.mult, ALU.mult)
                if isS:
                    S.dma("pool", o_ys, h[:64, 0, :], "out")
                else:
                    S.dma("pool", o_yp[ti * 512:(ti + 1) * 512, :].rearrange("(c p) d -> p c d", p=128), h, "out")


            if nxt is None:
                tail()
            else:
                deferred.append(tail)

        setup()
        for idx, tl in enumerate(TILES):
            run_tile(tl, idx, TILES[idx + 1] if idx + 1 < len(TILES) else None)
        S.wait_all_streams("pool")

        @block.sync
        def _(q):
            S.emit("sp", q)

        @block.gpsimd
        def _(q):
            S.emit("pool", q)

        @block.scalar
        def _(q):
            S.emit("act", q)

        @block.vector
        def _(q):
            S.emit("dve", q)

        @block.tensor
        def _(q):
            S.emit("pe", q)

    return nc


_NC = None


def _chunk_cols(w, ncol=256):
    K, N = w.shape
    nk, nq = K // 128, N // ncol
    return np.ascontiguousarray(w.reshape(nk, 128, nq, ncol).transpose(2, 1, 0, 3).reshape(nq, 128, nk * ncol))


def _chunk_wdown(w):
    wp = np.zeros((3072, D), np.float32)
    wp[:DFF] = w
    a = wp.reshape(6, 4, 128, 2, 512).transpose(3, 0, 2, 1, 4)
    return np.ascontiguousarray(a.reshape(12, 128, 2048))


def kernel(**inp):
    global _NC
    f = lambda a: np.ascontiguousarray(np.asarray(a, dtype=np.float32))
    if _NC is None:
        _NC = build_program()
    nc = _NC
    shared = {
        "g_mix": f(inp["g_mix"]).reshape(1, D), "w_in": _chunk_cols(f(inp["w_in"])[0]), "dw_a": f(inp["dw_a"])[0],
        "b_dw_a": f(inp["b_dw_a"]).reshape(1, D), "ln_a_g": f(inp["ln_a_g"]).reshape(1, D),
        "ln_a_b": f(inp["ln_a_b"]).reshape(1, D), "w_a_out": _chunk_cols(f(inp["w_a_out"])[0]),
        "ln_b_g": f(inp["ln_b_g"])[0], "ln_b_b": f(inp["ln_b_b"])[0], "w_s": f(inp["w_s"])[0],
        "b_s": f(inp["b_s"]).reshape(1, 8 * 128), "w_b_out": _chunk_cols(f(inp["w_b_out"])[0]), "w_out": _chunk_cols(f(inp["w_out"])[0]),
        "g_ffn": f(inp["g_ffn"]).reshape(1, D), "w_up": _chunk_cols(f(inp["w_up"])[0]), "dw_f": f(inp["dw_f"])[0],
        "b_dw_f": f(inp["b_dw_f"]).reshape(1, DFF), "w_down": _chunk_wdown(f(inp["w_down"])[0]), "g_final": f(inp["g_final"]),
    }
    xp, xsm = f(inp["x_prompt"]), f(inp["x_sample"])
    sa_, sf_ = f(inp["state_conv_a"])[0], f(inp["state_ffn_conv"])[0]
    in_maps = []
    for c in range(8):
        m = dict(shared)
        m["x_p"] = xp[c]
        m["x_s"] = xsm[16 * c:16 * c + 16].reshape(64, D)
        m["st_a"] = sa_[16 * c:16 * c + 16]
        m["st_f"] = sf_[16 * c:16 * c + 16].reshape(32, DFF)
        in_maps.append(m)
    res = run_bass_kernel_spmd(nc, in_maps, core_ids=list(range(8)))
    R = res.results
    y_p = np.stack([R[c]["o_yp"] for c in range(8)], 0)
    y_s = np.concatenate([R[c]["o_ys"].reshape(16, 4, D) for c in range(8)], 0)
    cap = np.stack([R[c]["o_cap"] for c in range(8)], 0)[None]
    cas = np.concatenate([R[c]["o_cas"] for c in range(8)], 0)[None]
    vs = np.concatenate([R[c]["o_vs"].reshape(16, 4, D) for c in range(8)], 0)[None]
    fp = np.stack([R[c]["o_fp"] for c in range(8)], 0)[None]
    fs = np.concatenate([R[c]["o_fs"].reshape(16, 2, DFF) for c in range(8)], 0)[None]
    return (y_p.astype(np.float32), y_s.astype(np.float32), cap.astype(np.float32), cas.astype(np.float32),
            vs.astype(np.float32), fp.astype(np.float32), fs.astype(np.float32))
```

```python
import numpy as np
import concourse.bass as bass
import concourse.mybir as mybir
from concourse.bass_utils import run_bass_kernel_spmd

F32 = mybir.dt.float32
BF16 = mybir.dt.bfloat16
AF = mybir.ActivationFunctionType
ALU = mybir.AluOpType

D = 1024
DFF = 2816
NIN = 6144
SEQ = 2048
NB_S = 16
TS = 4
CW = 31
EPS = 1e-6
BLK = 64


class Tok:
    __slots__ = ("sem", "val", "clock")

    def __init__(self, sem, val, clock):
        self.sem, self.val, self.clock = sem, val, clock


class Eng:
    def __init__(self, name, sem, is_pe=False):
        self.name, self.sem, self.is_pe = name, sem, is_pe
        self.items = []
        self.count = 0
        self.known = {}


class Sched:
    def __init__(self, nc, sems):
        self.nc = nc
        self.sems = sems
        self.eng = {n: Eng(n, n, n == "pe") for n in ("pe", "act", "dve", "pool", "sp")}
        self.W = {}
        self.R = {}
        self.stream_cnt = {}
        self.cache = {}
        self.pe_pending = False
        self.gs_next = 0
        self.qs_next = 0

    def blocks(self, ap):
        name = ap.tensor.name
        if name == "arena":
            sp = 0
        elif name == "ps":
            sp = 1
        else:
            return ()
        key = (sp, ap.offset, ap.ap, str(ap.dtype))
        r = self.cache.get(key)
        if r is not None:
            return r
        es = 2 if ap.dtype == BF16 else 4
        pat = ap.ap
        pstep = pat[0][0]
        off = ap.offset % pstep if pstep else ap.offset
        dims = [d for d in pat[1:] if d[1] > 1]
        if not dims:
            dims = [(1, 1)]
        outer = 1
        for d in dims[:-1]:
            outer *= d[1]
        res = set()
        if outer > 256:
            lo = off
            hi = off + sum(abs(s) * (c - 1) for s, c in dims) + 1
            res.update(range((lo * es) // BLK, (hi * es - 1) // BLK + 1))
        else:
            def rec(i, base):
                if i == len(dims) - 1:
                    s, c = dims[i]
                    lo = base
                    hi = base + s * (c - 1) + 1
                    res.update(range((lo * es) // BLK, (hi * es - 1) // BLK + 1))
                else:
                    s, c = dims[i]
                    for j in range(c):
                        rec(i + 1, base + j * s)
            rec(0, off)
        if sp == 1:
            res = set(b * BLK // 2048 for b in res)
        r = tuple((sp, b) for b in res)
        self.cache[key] = r
        return r

    def _gather(self, e, outs, ins, ignore=(), extra=()):
        need = {}

        def add(t):
            if t is None:
                return
            cur = need.get(t.sem)
            if cur is None or cur.val < t.val:
                need[t.sem] = t
        for ap in ins:
            for b in self.blocks(ap):
                add(self.W.get(b))
        for ap in outs:
            for b in self.blocks(ap):
                add(self.W.get(b))
                rr = self.R.get(b)
                if rr:
                    for t in rr.values():
                        add(t)
        for t in extra:
            add(t)
        for sem, t in need.items():
            if sem in ignore:
                continue
            if e.is_pe and sem == "pe":
                continue
            val = t.val
            if e.known.get(sem, 0) >= val:
                continue
            e.items.append(("wait", sem, val))
            e.known[sem] = val
            for s2, v2 in t.clock.items():
                if e.known.get(s2, 0) < v2:
                    e.known[s2] = v2

    def _publish(self, tok, outs, ins):
        for ap in ins:
            for b in self.blocks(ap):
                rr = self.R.get(b)
                if rr is None:
                    rr = self.R[b] = {}
                rr[tok.sem] = tok
        for ap in outs:
            for b in self.blocks(ap):
                self.W[b] = tok
                if b in self.R:
                    self.R[b] = {}

    def op(self, en, fn, outs, ins, inc=True, extra=()):
        e = self.eng[en]
        self._gather(e, outs, ins, extra=extra)
        if inc:
            e.count += 1
            val = e.count
            if e.is_pe:
                self.pe_pending = False
        else:
            assert e.is_pe
            val = e.count + 1
            self.pe_pending = True
        tok = Tok(e.sem, val, dict(e.known))
        e.items.append(("op", fn, e.sem if inc else None, 1))
        self._publish(tok, outs, ins)
        return tok

    def dma(self, en, out, in_, stream=None, ignore=(), extra=()):
        e = self.eng[en]
        if stream is None or not stream.startswith("w"):
            if en == "pool":
                stream = "q%d" % (self.qs_next % NQS)
                self.qs_next += 1
            else:
                stream = "g%d" % (self.gs_next % NGS)
                self.gs_next += 1
            prev = self.stream_cnt.get(stream, 0)
            if prev and e.known.get(stream, 0) < 16 * prev:
                e.items.append(("wait", stream, 16 * prev))
                e.known[stream] = 16 * prev
        self._gather(e, [out], [in_], ignore=ignore, extra=extra)
        n = self.stream_cnt.get(stream, 0) + 1
        self.stream_cnt[stream] = n
        tok = Tok(stream, 16 * n, dict(e.known))
        e.items.append(("op", lambda q, o=out, i=in_: q.dma_start(out=o, in_=i), stream, 16))
        self._publish(tok, [out], [in_])
        return tok

    def wait_all_streams(self, en):
        e = self.eng[en]
        for s, n in self.stream_cnt.items():
            if e.known.get(s, 0) < 16 * n:
                e.items.append(("wait", s, 16 * n))
                e.known[s] = 16 * n

    def emit(self, en, q):
        for it in self.eng[en].items:
            if it[0] == "wait":
                q.wait_ge(self.sems[it[1]], it[2])
            else:
                ins = it[1](q)
                if it[2] is not None:
                    ins.then_inc(self.sems[it[2]], it[3])


NGS = 24
NQS = 12
STREAMS = ["wst0", "wst1", "wst2"] + ["wbf%d" % i for i in range(11)] + ["g%d" % i for i in range(NGS)] + ["q%d" % i for i in range(NQS)]
ARENA_BYTES = 206 * 1024


def build_program(debug=False):
    nc = bass.Bass("TRN2", target_bir_lowering=False)

    def din(name, shape):
        return nc.dram_tensor(name, list(shape), F32, kind="ExternalInput").ap()

    def dout(name, shape):
        return nc.dram_tensor(name, list(shape), F32, kind="ExternalOutput").ap()

    x_p = din("x_p", (SEQ, D))
    x_s = din("x_s", (NB_S * TS, D))
    st_a = din("st_a", (NB_S, CW - 1, D))
    st_f = din("st_f", (NB_S * 2, DFF))
    g_mix = din("g_mix", (1, D)); w_in = din("w_in", (24, 128, 2048)); dw_a = din("dw_a", (CW, D))
    b_dw_a = din("b_dw_a", (1, D)); ln_a_g = din("ln_a_g", (1, D)); ln_a_b = din("ln_a_b", (1, D))
    w_a_out = din("w_a_out", (4, 128, 2048)); ln_b_g = din("ln_b_g", (D,)); ln_b_b = din("ln_b_b", (D,))
    w_s = din("w_s", (8, 128, 128)); b_s = din("b_s", (1, 8 * 128)); w_b_out = din("w_b_out", (4, 128, 2048))
    w_out = din("w_out", (4, 128, 2048)); g_ffn = din("g_ffn", (1, D)); w_up = din("w_up", (22, 128, 2048))
    dw_f = din("dw_f", (3, DFF)); b_dw_f = din("b_dw_f", (1, DFF)); w_down = din("w_down", (12, 128, 2048))
    g_final = din("g_final", (D,))

    wscr = nc.dram_tensor("wscr", [70, 128, 2048], BF16, kind="Internal").ap()
    o_yp = dout("o_yp", (SEQ, D)); o_ys = dout("o_ys", (NB_S * TS, D))
    o_cap = dout("o_cap", (CW - 1, D)); o_cas = dout("o_cas", (NB_S, CW - 1, D))
    o_vs = dout("o_vs", (NB_S * TS, D)); o_fp = dout("o_fp", (2, DFF)); o_fs = dout("o_fs", (NB_S * 2, DFF))

    import contextlib
    with contextlib.ExitStack() as es:
        arena = es.enter_context(nc.sbuf_tensor("arena", [128, ARENA_BYTES // 4], F32))
        ps = es.enter_context(nc.psum_tensor("ps", [128, 8 * 512], F32))
        sems = {}
        for n in ["pe", "act", "dve", "pool"] + STREAMS:
            sems[n] = es.enter_context(nc.semaphore("s_" + n))
        block = es.enter_context(nc.Block())
        S = Sched(nc, sems)

        cur = [0]
        named = {}

        def alloc(nbytes, name=None):
            if name is not None and name in named:
                return named[name]
            o = cur[0]
            cur[0] = (o + nbytes + 63) // 64 * 64
            assert cur[0] <= ARENA_BYTES, ("arena overflow", name, cur[0])
            if name is not None:
                named[name] = o
            return o

        def view(off, dt, shape):
            n = 1
            for s in shape[1:]:
                n *= s
            es_ = 2 if dt == BF16 else 4
            a = arena[:, off // 4: off // 4 + (n * es_ + 3) // 4]
            if dt == BF16:
                a = a.bitcast(BF16)
            if len(shape) == 3:
                a = a.rearrange("p (a b) -> p a b", a=shape[1])
            elif len(shape) == 4:
                a = a.rearrange("p (a b c) -> p a b c", a=shape[1], b=shape[2])
            if shape[0] < 128:
                a = a[:shape[0]]
            return a

        def mk(name, dt, shape, alias=None):
            n = 1
            for s in shape[1:]:
                n *= s
            nb = n * (2 if dt == BF16 else 4)
            off = named[alias] if alias else alloc(nb, name)
            return view(off, dt, shape)

        def psb(bank, dt=F32):
            a = ps[:, bank * 512:(bank + 1) * 512]
            return a.bitcast(BF16) if dt == BF16 else a

        h = mk("h", F32, (128, 4, D))
        xs = [mk("xs0", BF16, (128, D)), mk("xs1", BF16, (128, D))]
        xnT = mk("xnT", BF16, (128, 8, 512))
        sg = mk("sg", F32, (128, 2, 512))
        glu = mk("glu", BF16, (128, 8, 544))
        glu32 = mk("glu32", F32, (128, 8, 64))
        ybuf = mk("y", F32, (128, 8, 512))
        u = mk("u", BF16, (128, 8, 512))
        gates = mk("gates", BF16, (128, 16, 512))
        sa = mk("sa", BF16, (128, 8, 512))
        assert named["sa"] == named["gates"] + 16384
        vn = mk("vn", BF16, (128, 4, D))
        vtm = mk("vtm", F32, (128, 4, D))
        ytmp = mk("ytmp", BF16, (128, 4, 512))
        t0 = mk("t0", F32, (128, 2, 512))
        gtmp = mk("gtmp", F32, (128, 2, 512))
        atmp = mk("atmp", F32, (128, 2, 544))
        akeep = mk("akeep", F32, (128, 22, 32))
        diag = mk("diag", BF16, (128, 3, 8, 128))
        ghalo = mk("ghalo", BF16, (128, 8, 30))
        merged = mk("merged", BF16, (128, 8, 512), alias="glu")
        hh = mk("hh", BF16, (128, 22, 512), alias="gates")
        ostage = mk("ostage", F32, (128, DFF), alias="y")
        wst = [mk("wst%d" % i, F32, (128, 2048)) for i in range(3)]
        wbf = [mk("wbf%d" % i, BF16, (128, 2048)) for i in range(5)]
        wbf_late = wbf + [view(named["wst%d" % (q // 2)] + 4096 * (q % 2), BF16, (128, 2048)) for q in range(6)]
        gfin_bc = mk("gfin_bc", F32, (128, D))
        lnbg_bc = mk("lnbg_bc", F32, (128, D))
        lnbb_bc = mk("lnbb_bc", F32, (128, D))
        ident_f = mk("ident_f", F32, (128, 128))
        ident_b = mk("ident_b", BF16, (128, 128))
        mask_ut = mk("mask_ut", F32, (128, 128))
        WsT = mk("WsT", BF16, (128, 8, 128))
        WsTb = mk("WsTb", BF16, (64, 8, 64))
        bsrow = mk("bsrow", BF16, (1, 8, 128))
        bsrowS = mk("bsrowS", BF16, (1, 8, 64))
        ones_row = mk("ones_row", BF16, (1, 128))
        onesN = mk("onesN", BF16, (128, 128))
        colA = mk("colA", F32, (128, 8, 36))
        colF = mk("colF", F32, (128, 22, 4))
        stat = mk("stat", F32, (128, 32))
        nh8 = mk("nh8", F32, (128, 8))
        bnst = mk("bnst", F32, (128, 2, 6))
        mv = mk("mv", F32, (128, 4, 2))

        def T(en, fn, outs, ins, inc=True):
            return S.op(en, fn, outs, ins, inc)

        bank_rr = [0]
        bank_pool = [4]

        def rbank():
            b = bank_rr[0] % bank_pool[0]
            bank_rr[0] += 1
            return b

        def mm(out, lhsT, rhs, start, stop, inc):
            T("pe", lambda q, o=out, l=lhsT, r=rhs, s=start, p=stop: q.matmul(o, lhsT=l, rhs=r, start=s, stop=p),
              [out], [lhsT, rhs], inc=inc)

        def tr(out, in_, ident, inc):
            T("pe", lambda q, o=out, i=in_, d=ident: q.transpose(out=o, in_=i, identity=d), [out], [in_, ident], inc=True)

        def act(out, in_, func, scale=1.0, bias=0.0, accum=None):
            ins = [in_] + [a for a in (scale, bias) if not isinstance(a, float)]
            outs = [out] + ([accum] if accum is not None else [])
            if accum is not None:
                T("act", lambda q: q.activation(out=out, in_=in_, func=func, scale=scale, bias=bias, accum_out=accum), outs, ins)
            else:
                T("act", lambda q: q.activation(out=out, in_=in_, func=func, scale=scale, bias=bias), outs, ins)

        def tt(en, out, in0, in1, op):
            T(en, lambda q: q.tensor_tensor(out=out, in0=in0, in1=in1, op=op), [out], [in0, in1])

        def tsc(en, out, in0, s1, s2, op0, op1=None):
            ins = [in0] + [a for a in (s1, s2) if a is not None and not isinstance(a, float)]
            if op1 is None:
                T(en, lambda q: q.tensor_scalar(out=out, in0=in0, scalar1=s1, scalar2=None, op0=op0), [out], ins)
            else:
                T(en, lambda q: q.tensor_scalar(out=out, in0=in0, scalar1=s1, scalar2=s2, op0=op0, op1=op1), [out], ins)

        def stt(out, in0, scalar, in1, op0, op1):
            ins = [in0, in1] + ([scalar] if not isinstance(scalar, float) else [])
            T("dve", lambda q: q.scalar_tensor_tensor(out=out, in0=in0, scalar=scalar, in1=in1, op0=op0, op1=op1), [out], ins)

        def recip(out, in_):
            T("dve", lambda q: q.reciprocal(out=out, in_=in_), [out], [in_])

        def cp(en, out, in_):
            if en == "act":
                act(out, in_, AF.Identity)
            else:
                T(en, lambda q: q.tensor_copy(out=out, in_=in_), [out], [in_])

        def mset(out, v):
            T("pool", lambda q: q.memset(out, v), [out], [])

        dbg_list = []

        def dbg(name, ap, tl=None):
            if not debug:
                return
            shp = list(ap.shape)
            if ap.dtype == BF16:
                n = 1
                for q_ in shp[1:]:
                    n *= q_
                assert n <= 1024
                sc = vtm[:shp[0], 2, 0:n]
                if len(shp) == 3:
                    sc = sc.rearrange("p (a b) -> p a b", a=shp[1])
                cp("dve", sc, ap)
                ap = sc
            o = nc.dram_tensor("dbg_" + name, shp, F32, kind="ExternalOutput").ap()
            S.dma("pool", o, ap, "out")
            dbg_list.append(name)

        wq = []
        wkind = []
        CONV_ENG = {"gate": "dve", "val": "dve", "v": "act", "u": "act", "g": "dve", "wb": "act", "wa": "act",
                    "wo": "act", "ua": "dve", "ub": "act", "wd": "dve"}
        wstate = {"loaded": 0}

        def wview(i):
            src, nk, ncol = wq[i]
            stv = wst[i % 3][:, 0:nk * ncol]
            bfv = (wbf[i % 5] if i < NCH_TILE else wbf_late[(5 + i - NCH_TILE) % 11])[:, 0:nk * ncol]
            return src, stv, bfv.rearrange("p (k n) -> p k n", k=nk), bfv

        scr_tok = {}

        def wensure(upto):
            upto = min(upto, len(wq) - 1)
            while wstate["loaded"] <= upto:
                i = wstate["loaded"]
                src, stv, _, bff = wview(i)
                nk, ncol = wq[i][1], wq[i][2]
                n32 = nk * ncol // 2
                loc = i % NCH_TILE
                bf32 = view(named["wbf%d" % (i % 5)], F32, (128, 1024))[:, 0:n32]
                if i < NCH_TILE:
                    S.dma("sp", stv, src, "wst%d" % (i % 3))
                    ce = CONV_ENG[wkind[i]]
                    if ce == "alt":
                        ce = "act" if i % 2 else "dve"
                    cp(ce, bff, stv)
                    scr_tok[loc] = S.dma("pool", wscr[loc, :, 0:2 * n32], bff)
                else:
                    S.dma("sp", bff, wscr[loc, :, 0:2 * n32], "wbf%d" % ((5 + i - NCH_TILE) % 11), extra=[scr_tok[loc]])
                wstate["loaded"] += 1

        def wget(i, ahead=3):
            if ahead == 3 and i >= NCH_TILE:
                ahead = 7
            wensure(i + ahead)
            return wview(i)[2]

        def wsrc(w, ci, nk=8, ncol=256):
            return (w[ci, :, 0:nk * ncol], nk, ncol)

        def tile_wseq():
            seq = []
            for i in range(4):
                seq.append(("gate", wsrc(w_in, 4 + i)))
                seq.append(("val", wsrc(w_in, i)))
            for i in range(4):
                seq.append(("v", wsrc(w_in, 12 + i)))
            for i in range(4):
                seq.append(("u", wsrc(w_in, 8 + i)))
            for i in range(8):
                seq.append(("g", wsrc(w_in, 16 + i)))
            for i in range(4):
                seq.append(("wb", wsrc(w_b_out, i)))
                seq.append(("wa", wsrc(w_a_out, i)))
            for i in range(4):
                seq.append(("wo", wsrc(w_out, i)))
            for i in range(11):
                seq.append(("ua", wsrc(w_up, i)))
                seq.append(("ub", wsrc(w_up, 11 + i)))
            for hf in range(2):
                for kg in range(6):
                    nk = 4 if kg < 5 else 2
                    seq.append(("wd", wsrc(w_down, hf * 6 + kg, nk=nk, ncol=512)))
            return seq

        NCH_TILE = 70
        TILES = [("P", i) for i in range(4)] + [("S", 0)]
        wbase = {}
        for tl in TILES:
            wbase[tl] = len(wq)
            for kind, src in tile_wseq():
                wq.append(src)
                wkind.append(kind)

        cctr = [0]

        def cdma(out, in_):
            S.dma("sp", out, in_)

        def setup():
            mset(ident_f, 0.0)
            T("pool", lambda q: q.affine_select(out=ident_f, in_=ident_f, pattern=[[-1, 128]], compare_op=ALU.not_equal,
                                                fill=1.0, base=0, channel_multiplier=1), [ident_f], [ident_f])
            cp("pool", ident_b, ident_f)
            mset(mask_ut, 1.0)
            T("pool", lambda q: q.affine_select(out=mask_ut, in_=mask_ut, pattern=[[1, 128]], compare_op=ALU.is_ge,
                                                fill=0.0, base=0, channel_multiplier=-1), [mask_ut], [mask_ut])
            mset(ones_row, 1.0)
            mset(nh8, -0.5)
            mset(onesN, 1.0 / D)
            mset(WsTb, 0.0)
            mset(akeep, 0.0)
            stg = vtm[:, 0, :]
            cdma(stg[0:31, :], dw_a)
            cdma(stg[31:32, :], b_dw_a)
            cdma(stg[32:33, :], ln_a_g)
            cdma(stg[33:34, :], ln_a_b)
            cdma(stg[34:35, :], g_mix)
            cdma(stg[35:36, :], g_ffn)
            S.dma("sp", h, x_p[0:512, :].rearrange("(c p) d -> p c d", p=128), "x")
            wensure(2)
            b = rbank()
            for j in range(8):
                tr(psb(b)[:, j * 36:(j + 1) * 36], stg[0:36, j * 128:(j + 1) * 128], ident_f[0:36, 0:36], inc=(j == 7))
            cp("dve", colA.rearrange("p a b -> p (a b)"), psb(b)[:, 0:288])
            cdma(gfin_bc, g_final.partition_broadcast(128))
            cdma(lnbg_bc, ln_b_g.partition_broadcast(128))
            cdma(lnbb_bc, ln_b_b.partition_broadcast(128))

        def setup_c():
            stgf = ostage
            cdma(stgf[0:3, :], dw_f)
            cdma(stgf[3:4, :], b_dw_f)
            b = rbank()
            for j in range(22):
                tr(psb(b)[:, j * 4:(j + 1) * 4], stgf[0:4, j * 128:(j + 1) * 128], ident_f[0:4, 0:4], inc=(j == 21))
            cp("dve", colF.rearrange("p a b -> p (a b)"), psb(b)[:, 0:88])
        def setup_b():
            wsn = vtm[:, 1, :].rearrange("p (h s) -> p h s", h=8)
            cdma(wsn, w_s.rearrange("h t s -> t h s"))
            for hh_ in range(8):
                b = rbank()
                tr(psb(b)[:, 0:128], wsn[:, hh_, :], ident_f, inc=True)
                tt("dve", WsT[:, hh_, :], psb(b)[:, 0:128], mask_ut, ALU.mult)
            bsf = vtm[0:1, 2, :]
            cdma(bsf, b_s)
            cp("dve", bsrow.rearrange("p h t -> p (h t)"), bsf)
            bsv = bsf.rearrange("p (h t) -> p h t", h=8)
            for bb in range(NB_S):
                cp("dve", bsrowS[:, :, 4 * bb:4 * bb + 4], bsv[:, :, 0:4])

        def setup_b2():
            btoks = []
            for bb in range(NB_S):
                btoks.append(S.dma("sp", WsTb[4 * bb:4 * bb + 4, :, 4 * bb:4 * bb + 4], WsT[0:4, :, 0:4], "blk",
                                   ignore=set(t.sem for t in btoks)))
            S.op("dve", lambda q: q.tensor_copy(out=WsTb[0:1, 0, 0:2], in_=WsTb[0:1, 0, 0:2]), [WsTb], [], extra=btoks)

        def norm_front(hb, nch, cs):
            for c in range(nch):
                act(xs[c % 2][:cs], hb[:cs, c, :], AF.Square, accum=stat[:cs, c:c + 1])
            act(stat[:cs, 4:4 + nch], stat[:cs, 0:nch], AF.Sqrt, scale=1.0 / D, bias=EPS)
            recip(stat[:cs, 8:8 + nch], stat[:cs, 4:4 + nch])

        def norm_scale(hb, c, cs):
            tsc("dve", xs[c % 2][:cs], hb[:cs, c, :], stat[:cs, 8 + c:9 + c], None, ALU.mult)

        def norm_back(hb, nch, cs, TT, gidx, bank0, prescaled=0):
            for c in range(nch):
                if c >= prescaled:
                    norm_scale(hb, c, cs)
                for k in range(8):
                    o = psb(bank0 + k // 2, BF16)[:, (k % 2) * 512 + c * 128:(k % 2) * 512 + c * 128 + cs]
                    tr(o, xs[c % 2][:cs, k * 128:(k + 1) * 128], ident_b[:cs, :cs], inc=(k == 7))
            for k in range(8):
                i_ = psb(bank0 + k // 2, BF16)[:, (k % 2) * 512:(k % 2) * 512 + TT]
                act(xnT[:, k, 0:TT], i_, AF.Identity, scale=colA[:, k, gidx:gidx + 1])

        def norm_to_T(hb, nch, cs, TT, gidx):
            norm_front(hb, nch, cs)
            norm_back(hb, nch, cs, TT, gidx, 4)

        HV = [h, vtm]
        deferred = []

        def run_tile(tl, idx, nxt):
            kind, ti = tl
            h = HV[idx % 2]
            vtm = HV[1 - idx % 2]
            isS = kind == "S"
            nch, cs, TT = (1, 64, 64) if isS else (4, 128, 512)
            last = (not isS) and ti == 3
            wi = [wbase[tl]]

            def nextw(ahead=3):
                v = wget(wi[0], ahead)
                wi[0] += 1
                return v

            if isS:
                gfull = lambda j: glu[:, j, :].rearrange("p (b t) -> p b t", t=34)
                gnew = lambda j: gfull(j)[:, :, 30:34]
                gtap = lambda j, k: gfull(j)[:, :, k:k + 4]
                as3 = lambda a: a.rearrange("p (b t) -> p b t", t=4)
            else:
                gnew = lambda j: glu[:, j, 30:30 + TT]
                gtap = lambda j, k: glu[:, j, k:k + TT]
                as3 = lambda a: a

            if isS:
                if idx == 0:
                    S.dma("sp", h[:64, 0, :], x_s, "x")
            else:
                pass
                if ti == 0:
                    mset(glu[:, :, 0:30], 0.0)
                    mset(akeep, 0.0)
                else:
                    cp("pool", glu[:, :, 0:30], ghalo)

            def s_history():
                for g4 in range(4):
                    stg = vtm[0:120, g4, :]
                    S.dma("sp", stg, st_a[4 * g4:4 * g4 + 4].rearrange("b j c -> (b j) c"), "hist")
                    for ct in range(8):
                        b = rbank()
                        tr(psb(b)[:, 0:120], stg[:, ct * 128:(ct + 1) * 128], ident_f[0:120, 0:120], inc=True)
                        cp("act" if ct % 2 else "dve", gfull(ct)[:, 4 * g4:4 * g4 + 4, 0:30],
                           psb(b)[:, 0:120].rearrange("p (b j) -> p b j", j=30))
                stg = ostage[0:32, :]
                S.dma("sp", stg, st_f, "hist")
                for m in range(22):
                    b = rbank()
                    tr(psb(b)[:, 0:32], stg[:, m * 128:(m + 1) * 128], ident_f[0:32, 0:32], inc=True)
                    cp("act" if m % 2 else "dve", akeep[:, m, :], psb(b)[:, 0:32])

            if idx == 0:
                norm_to_T(h, nch, cs, TT, 34)

            groups = [] if isS else [(j, g0, min(8, CW - g0)) for j in range(8) for g0 in range(0, CW, 8)]

            def build_group(gi):
                j, g0, ng = groups[gi]
                ds = gi % 3
                tt("dve", diag[:, ds, 0:ng, :], ident_b.unsqueeze(1).broadcast_to([128, ng, 128]),
                   colA[:, j, g0:g0 + ng].unsqueeze(2).broadcast_to([128, ng, 128]), ALU.mult)

            if groups:
                build_group(0)
                build_group(1)

            bank_pool[0] = 8
            for i in range(4):
                wg = nextw()
                wv = nextw()
                for mloc in range(2):
                    j = 2 * i + mloc
                    bg, bv = rbank(), rbank()
                    for k in range(8):
                        mm(psb(bg)[:, 0:TT], wg[:, k, mloc * 128:(mloc + 1) * 128], xnT[:, k, 0:TT], k == 0, k == 7, k == 7)
                    for k in range(8):
                        mm(psb(bv)[:, 0:TT], wv[:, k, mloc * 128:(mloc + 1) * 128], xnT[:, k, 0:TT], k == 0, k == 7, k == 7)
                    act(sg[:, j % 2, 0:TT], psb(bg)[:, 0:TT], AF.Sigmoid)
                    tt("dve", gnew(j), as3(psb(bv)[:, 0:TT]), as3(sg[:, j % 2, 0:TT]), ALU.mult)
                    if isS:
                        tt("dve", glu32[:, j, 0:64], psb(bv)[:, 0:64], sg[:, j % 2, 0:64], ALU.mult)
                    elif last:
                        tt("dve", glu32[:, j, 0:64], psb(bv)[:, 448:512], sg[:, j % 2, 448:512], ALU.mult)

            bank_pool[0] = 4
            if idx == 0:
                setup_c()
            while deferred:
                deferred.pop(0)()
            if isS:
                s_history()
            if isS or last:
                stg = vtm[0:64, 3, :]
                for half in range(2):
                    b = rbank()
                    for jj in range(4):
                        j = half * 4 + jj
                        tr(psb(b)[0:64, jj * 128:(jj + 1) * 128], glu32[:, j, 0:64], ident_f, inc=(jj == 3))
                    cp("act", stg[:, half * 512:(half + 1) * 512], psb(b)[0:64, :])
                if isS:
                    for bb in range(NB_S):
                        S.dma("pool", o_cas[bb, 26:30, :], stg[4 * bb:4 * bb + 4, :], "out")
                    S.dma("pool", o_cas[:, 0:26, :], st_a[:, 4:30, :], "out")
                else:
                    S.dma("pool", o_cap, stg[34:64, :], "out")

            def conv_stats(j):
                slot = j % 2
                act(ytmp[:, 2 + slot, 0:TT], ybuf[:, j, 0:TT], AF.Square)
                cp("pool", ytmp[:, slot, 0:TT], ybuf[:, j, 0:TT])
                mm(psb(6)[:, 0:TT], onesN, ytmp[:, slot, 0:TT], j == 0, j == 7, True)
                mm(psb(7)[:, 0:TT], onesN, ytmp[:, 2 + slot, 0:TT], j == 0, j == 7, True)

            dcount = [0]
            s_conv = []
            if isS:
                def mk_conv(j, k):
                    def f():
                        yv = as3(ybuf[:, j, 0:TT])
                        if k == 0:
                            tsc("dve", yv, gtap(j, 0), colA[:, j, 0:1], colA[:, j, 31:32], ALU.mult, ALU.add)
                        else:
                            stt(yv, gtap(j, k), colA[:, j, k:k + 1], yv, ALU.mult, ALU.add)
                    return f
                for k in range(CW):
                    for j in range(8):
                        s_conv.append(mk_conv(j, k))

            def pump(n):
                for _ in range(min(n, len(s_conv))):
                    s_conv.pop(0)()

            conv_st = {"gi": 0, "cb": None}

            def conv_ctile(j):
                for g0 in range(0, CW, 8):
                    gi = conv_st["gi"]
                    jj, g0_, ng = groups[gi]
                    assert jj == j and g0_ == g0
                    if g0 == 0:
                        conv_st["cb"] = rbank()
                    cb = conv_st["cb"]
                    if gi + 2 < len(groups):
                        build_group(gi + 2)
                    for kk in range(ng):
                        k = g0 + kk
                        mm(as3(psb(cb)[:, 0:TT]), diag[:, gi % 3, kk, :], gtap(j, k), k == 0, k == CW - 1, kk == ng - 1)
                    if g0 + ng == CW:
                        if j >= 1:
                            conv_stats(j - 1)
                        act(ybuf[:, j, 0:TT], psb(cb)[:, 0:TT], AF.Identity, bias=colA[:, j, 31:32])
                    conv_st["gi"] += 1

            interleave = (idx == 0) and not isS
            if not isS and not interleave:
                for j in range(8):
                    conv_ctile(j)
                conv_stats(7)
            if idx == 0:
                setup_b()
            if idx == 1:
                setup_b2()
            if not isS and not last:
                cp("pool", ghalo, glu[:, :, 512:542])
            if isS:
                dbg("y", ybuf[:, :, 0:64])
                dbg("xnT", xnT[:, :, 0:64])

            def ln_a_dve():
                mean_sb, rstd_sb = t0[:, 0, 0:TT], t0[:, 1, 0:TT]
                tmpv = gtmp[:, 0, 0:TT]
                act(mean_sb, psb(6)[:, 0:TT], AF.Identity)
                tt("dve", tmpv, mean_sb, mean_sb, ALU.mult)
                tt("dve", tmpv, psb(7)[:, 0:TT], tmpv, ALU.subtract)
                act(tmpv, tmpv, AF.Sqrt, bias=EPS)
                recip(rstd_sb, tmpv)
                for j in range(8):
                    en_ = "pool" if j >= 4 else "dve"
                    tt(en_, ybuf[:, j, 0:TT], ybuf[:, j, 0:TT], mean_sb, ALU.subtract)
                    tt(en_, ybuf[:, j, 0:TT], ybuf[:, j, 0:TT], rstd_sb, ALU.mult)


            if not isS and not interleave:
                ln_a_dve()

            vb = [4, 5, 0, 1]
            for pair in range(2):
                ws = [nextw(), nextw()]
                for c in range(nch):
                    bk = vb[c] if pair == 0 else [2, 3, 4, 5][c]
                    for q in range(2):
                        for k in range(8):
                            mm(psb(bk)[:cs, q * 256:(q + 1) * 256], xnT[:, k, c * cs:(c + 1) * cs], ws[q][:, k, :],
                               k == 0, k == 7, k == 7)
                    act(vtm[:cs, c, pair * 512:(pair + 1) * 512], psb(bk)[:cs, :], AF.Gelu)
                pump(32)
            for c in range(nch):
                for q in range(2):
                    T("dve", lambda e, c=c, q=q: e.bn_stats(out=bnst[:cs, q, :], in_=vtm[:cs, c, q * 512:(q + 1) * 512]),
                      [bnst[:cs, q, :]], [vtm[:cs, c, q * 512:(q + 1) * 512]])
                T("dve", lambda e, c=c: e.bn_aggr(out=mv[:cs, c, :], in_=bnst[:cs].rearrange("p a b -> p (a b)")), [mv[:cs, c, :]], [bnst[:cs]])
            tsc("dve", stat[:cs, 12:12 + nch], mv[:cs, 0:nch, 1], EPS, None, ALU.add)
            tt("pool", stat[:cs, 16:16 + nch], stat[:cs, 12:12 + nch], nh8[:cs, 0:nch], ALU.pow)
            def ln_b_apply(chunks=None):
                for c in (range(nch) if chunks is None else chunks):
                    stt(vtm[:cs, c, :], vtm[:cs, c, :], mv[:cs, c, 0:1], lnbg_bc[:cs], ALU.subtract, ALU.mult)
                    if isS:
                        stt(vtm[:cs, 1, :], vtm[:cs, c, :], stat[:cs, 16 + c:17 + c], lnbb_bc[:cs], ALU.mult, ALU.add)
                        cp("dve", vn[:cs, c, :], vtm[:cs, 1, :])
                        S.dma("pool", o_vs, vtm[:cs, 1, :], "out")
                    else:
                        stt(vn[:cs, c, :], vtm[:cs, c, :], stat[:cs, 16 + c:17 + c], lnbb_bc[:cs], ALU.mult, ALU.add)

            bank_pool[0] = 6 if interleave else 8
            for i in range(4):
                w = nextw()
                for mloc in range(2):
                    j = 2 * i + mloc
                    b = rbank()
                    for k in range(8):
                        mm(psb(b)[:, 0:TT], w[:, k, mloc * 128:(mloc + 1) * 128], xnT[:, k, 0:TT], k == 0, k == 7, k == 7)
                    act(u[:, j, 0:TT], psb(b)[:, 0:TT], AF.Gelu)
                pump(16)
                if interleave:
                    conv_ctile(i)
            for j in (range(0) if (isS or interleave) else range(8)):
                act(sa[:, j, 0:TT], ybuf[:, j, 0:TT], AF.Silu, scale=colA[:, j, 32:33], bias=colA[:, j, 33:34])
            for i in range(8):
                w = nextw()
                for mloc in range(2):
                    j = 2 * i + mloc
                    b = rbank()
                    for k in range(8):
                        mm(psb(b)[:, 0:TT], w[:, k, mloc * 128:(mloc + 1) * 128], xnT[:, k, 0:TT], k == 0, k == 7, k == 7)
                    act(gates[:, j, 0:TT], psb(b)[:, 0:TT], AF.Sigmoid)
                pump(16)
                if interleave and i < 4:
                    conv_ctile(4 + i)
                    ln_b_apply([i])
                if interleave and i == 3:
                    conv_stats(7)
                    ln_a_dve()
            if interleave:
                for j_ in range(8):
                    act(sa[:, j_, 0:TT], ybuf[:, j_, 0:TT], AF.Silu, scale=colA[:, j_, 32:33], bias=colA[:, j_, 33:34])
            if isS:
                pump(len(s_conv))
                for j in range(8):
                    conv_stats(j)
                ln_a_dve()
                for j in range(8):
                    act(sa[:, j, 0:TT], ybuf[:, j, 0:TT], AF.Silu, scale=colA[:, j, 32:33], bias=colA[:, j, 33:34])

            if isS:
                dbg("sa", sa[:, :, 0:64])
                dbg("u", u[:, :, 0:64])
                dbg("gates", gates[:, :, 0:64])
                dbg("vn", vn[:64, 0, :])
            if not interleave:
                ln_b_apply()
            bank_pool[0] = 8
            for hd in range(8):
                b = rbank()
                for c in range(nch):
                    o = psb(b)[:, c * 128:c * 128 + cs]
                    if isS:
                        mm(o, vn[:cs, c, hd * 128:(hd + 1) * 128], WsTb[:, hd, :], True, False, False)
                        mm(o, ones_row, bsrowS[:, hd, :], False, True, True)
                    else:
                        mm(o, vn[:cs, c, hd * 128:(hd + 1) * 128], WsT[:, hd, :], True, False, False)
                        mm(o, ones_row, bsrow[:, hd, :], False, True, True)
                tt("dve", u[:, hd, 0:TT], u[:, hd, 0:TT], psb(b)[:, 0:TT], ALU.mult)

            if isS:
                dbg("prod", u[:, :, 0:64])
            ta, tb = sg[:, 0, 0:TT], sg[:, 1, 0:TT]
            for i in range(4):
                wb_ = nextw()
                wa_ = nextw()
                for mloc in range(2):
                    j = 2 * i + mloc
                    bb_, ba_ = rbank(), rbank()
                    for k in range(8):
                        mm(psb(bb_)[:, 0:TT], wb_[:, k, mloc * 128:(mloc + 1) * 128], u[:, k, 0:TT], k == 0, k == 7, k == 7)
                    for k in range(8):
                        mm(psb(ba_)[:, 0:TT], wa_[:, k, mloc * 128:(mloc + 1) * 128], sa[:, k, 0:TT], k == 0, k == 7, k == 7)
                    tt("dve", tb, psb(bb_)[:, 0:TT], gates[:, 8 + j, 0:TT], ALU.mult)
                    tt("dve", ta, psb(ba_)[:, 0:TT], gates[:, j, 0:TT], ALU.mult)
                    tt("dve", merged[:, j, 0:TT], ta, tb, ALU.add)

            bank_pool[0] = 4
            wos = [nextw(0), nextw(0), nextw(0), nextw(1)]

            def n2_transposes(c):
                for k in range(8):
                    o = psb(4 + k // 2, BF16)[:, (k % 2) * 512 + c * 128:(k % 2) * 512 + c * 128 + cs]
                    tr(o, xs[c % 2][:cs, k * 128:(k + 1) * 128], ident_b[:cs, :cs], inc=(k == 7))

            for c in range(nch):
                for i in range(4):
                    b = rbank()
                    for k in range(8):
                        mm(psb(b)[:cs, 0:256], merged[:, k, c * cs:(c + 1) * cs], wos[i][:, k, :], k == 0, k == 7, k == 7)
                    tt("dve", h[:cs, c, i * 256:(i + 1) * 256], psb(b)[:cs, 0:256], h[:cs, c, i * 256:(i + 1) * 256], ALU.add)
                act(xs[c % 2][:cs], h[:cs, c, :], AF.Square, accum=stat[:cs, c:c + 1])
                act(stat[:cs, 4 + c:5 + c], stat[:cs, c:c + 1], AF.Sqrt, scale=1.0 / D, bias=EPS)
                recip(stat[:cs, 8 + c:9 + c], stat[:cs, 4 + c:5 + c])
                tsc("dve", xs[c % 2][:cs], h[:cs, c, :], stat[:cs, 8 + c:9 + c], None, ALU.mult)
                if c >= 1:
                    n2_transposes(c - 1)
            n2_transposes(nch - 1)
            for k in range(8):
                i_ = psb(4 + k // 2, BF16)[:, (k % 2) * 512:(k % 2) * 512 + TT]
                act(xnT[:, k, 0:TT], i_, AF.Identity, scale=colA[:, k, 35:36])
            bank_pool[0] = 8

            if isS:
                dbg("merged", merged[:, :, 0:64])
                dbg("h1", h[:64, 0, :])

            if isS:
                a3 = lambda s: atmp[:, s, 0:96].rearrange("p (b t) -> p b t", t=6)
                anew = lambda s: a3(s)[:, :, 2:6]
                atap = lambda s, k: a3(s)[:, :, k:k + 4]
                ahalo = lambda s: a3(s)[:, :, 0:2]
                akv = lambda m: akeep[:, m, :].rearrange("p (b t) -> p b t", t=2)
                alast = lambda s: a3(s)[:, :, 4:6]
            else:
                anew = lambda s: atmp[:, s, 2:2 + TT]
                atap = lambda s, k: atmp[:, s, k:k + TT]
                ahalo = lambda s: atmp[:, s, 0:2]
                akv = lambda m: akeep[:, m, 0:2]
                alast = lambda s: atmp[:, s, TT:TT + 2]
            if isS:
                apl = lambda b_: as3(psb(b_)[:, 0:TT])[:, :, 2:4]
            else:
                apl = lambda b_: psb(b_)[:, TT - 2:TT]

            def ffn_stage2(m, s, bb_):
                tv = t0[:, s, 0:TT]
                stt(as3(tv), atap(s, 1), colF[:, m, 1:2], as3(tv), ALU.mult, ALU.add)
                stt(as3(tv), atap(s, 0), colF[:, m, 0:1], as3(tv), ALU.mult, ALU.add)
                act(gtmp[:, s, 0:TT], tv, AF.Gelu)
                tt("dve", hh[:, m, 0:TT], gtmp[:, s, 0:TT], psb(bb_)[:, 0:TT], ALU.mult)

            if nxt is not None:
                tn = nxt[1]
                if nxt[0] == "S":
                    S.dma("sp", vtm[:64, 0, :], x_s, "x")
                else:
                    S.dma("sp", vtm, x_p[tn * 512:(tn + 1) * 512, :].rearrange("(c p) d -> p c d", p=128), "x")
            n_nch, n_cs = ((1, 64) if (nxt is not None and nxt[0] == "S") else (4, 128))
            pend = None
            for i in range(11):
                wa_ = nextw()
                wb_ = nextw()
                for mloc in range(2):
                    m = 2 * i + mloc
                    s = m % 2
                    ba_, bb_ = (2 * m) % 8, (2 * m + 1) % 8
                    for k in range(8):
                        mm(psb(ba_)[:, 0:TT], wa_[:, k, mloc * 128:(mloc + 1) * 128], xnT[:, k, 0:TT], k == 0, k == 7, k == 7)
                    for k in range(8):
                        mm(psb(bb_)[:, 0:TT], wb_[:, k, mloc * 128:(mloc + 1) * 128], xnT[:, k, 0:TT], k == 0, k == 7, k == 7)
                    cp("dve", ahalo(s), akv(m))
                    act(anew(s), as3(psb(ba_)[:, 0:TT]), AF.Identity)
                    act(akv(m), apl(ba_), AF.Identity)
                    act(as3(t0[:, s, 0:TT]), as3(psb(ba_)[:, 0:TT]), AF.Identity, scale=colF[:, m, 2:3], bias=colF[:, m, 3:4])
                    if pend is not None:
                        ffn_stage2(*pend)
                    pend = (m, s, bb_)
            ffn_stage2(*pend)

            if isS:
                dbg("colA", colA)
                dbg("colF", colF)
                dbg("stat", stat)
                dbg("xn2T", xnT[:, :, 0:64])
                dbg("hh", hh[:, 0:16, 0:64])
                dbg("akeep", akeep)
            if isS or last:
                nr = 32 if isS else 2
                for g in range(6):
                    b = rbank()
                    nm = 4 if g < 5 else 2
                    for mm_ in range(nm):
                        m = 4 * g + mm_
                        tr(psb(b)[0:nr, mm_ * 128:(mm_ + 1) * 128], akeep[:, m, 0:nr], ident_f, inc=(mm_ == nm - 1))
                    cp("act", ostage[0:nr, g * 512:g * 512 + nm * 128], psb(b)[0:nr, 0:nm * 128])
                S.dma("pool", o_fs if isS else o_fp, ostage[0:nr, :], "out")

            bank_pool[0] = 4
            if nxt is not None:
                norm_front(vtm, n_nch, n_cs)
                for c_ in range(min(2, n_nch)):
                    norm_scale(vtm, c_, n_cs)
            for hf in range(2):
                wb0 = 4 if hf == 0 else 0
                for kg in range(6):
                    w = nextw()
                    nk = 4 if kg < 5 else 2
                    for c in range(nch):
                        for kk in range(nk):
                            k = 4 * kg + kk
                            mm(psb(wb0 + c)[:cs, :], hh[:, k, c * cs:(c + 1) * cs], w[:, kk, :], k == 0, k == 21,
                               (kk == nk - 1))
                    if hf == 1 and kg == 2 and nxt is not None:
                        norm_back(vtm, n_nch, n_cs, n_nch * n_cs, 34, 4, prescaled=min(2, n_nch))
                for c in range(nch):
                    tt("dve", h[:cs, c, hf * 512:(hf + 1) * 512], psb(wb0 + c)[:cs, :], h[:cs, c, hf * 512:(hf + 1) * 512], ALU.add)

            if isS:
                dbg("h2", h[:64, 0, :])
            def tail():
                for c in range(nch):
                    act(xs[c % 2][:cs], h[:cs, c, :], AF.Square, accum=stat[:cs, 20 + c:21 + c])
                act(stat[:cs, 24:24 + nch], stat[:cs, 20:20 + nch], AF.Sqrt, scale=1.0 / D, bias=EPS)
                recip(stat[:cs, 28:28 + nch], stat[:cs, 24:24 + nch])
                for c in range(nch):
                    stt(h[:cs, c, :], h[:cs, c, :], stat[:cs, 28 + c:29 + c], gfin_bc[:cs], ALU.mult, ALU.mult)
                if isS:
                    S.dma("pool", o_ys, h[:64, 0, :], "out")
                else:
                    S.dma("pool", o_yp[ti * 512:(ti + 1) * 512, :].rearrange("(c p) d -> p c d", p=128), h, "out")


            if nxt is None:
                tail()
            else:
                deferred.append(tail)

        setup()
        for idx, tl in enumerate(TILES):
            run_tile(tl, idx, TILES[idx + 1] if idx + 1 < len(TILES) else None)
        S.wait_all_streams("pool")

        @block.sync
        def _(q):
            S.emit("sp", q)

        @block.gpsimd
        def _(q):
            S.emit("pool", q)

        @block.scalar
        def _(q):
            S.emit("act", q)

        @block.vector
        def _(q):
            S.emit("dve", q)

        @block.tensor
        def _(q):
            S.emit("pe", q)

    return nc


_NC = None


def _chunk_cols(w, ncol=256):
    K, N = w.shape
    nk, nq = K // 128, N // ncol
    return np.ascontiguousarray(w.reshape(nk, 128, nq, ncol).transpose(2, 1, 0, 3).reshape(nq, 128, nk * ncol))


def _chunk_wdown(w):
    wp = np.zeros((3072, D), np.float32)
    wp[:DFF] = w
    a = wp.reshape(6, 4, 128, 2, 512).transpose(3, 0, 2, 1, 4)
    return np.ascontiguousarray(a.reshape(12, 128, 2048))


def kernel(**inp):
    global _NC
    f = lambda a: np.ascontiguousarray(np.asarray(a, dtype=np.float32))
    if _NC is None:
        _NC = build_program()
    nc = _NC
    shared = {
        "g_mix": f(inp["g_mix"]).reshape(1, D), "w_in": _chunk_cols(f(inp["w_in"])[0]), "dw_a": f(inp["dw_a"])[0],
        "b_dw_a": f(inp["b_dw_a"]).reshape(1, D), "ln_a_g": f(inp["ln_a_g"]).reshape(1, D),
        "ln_a_b": f(inp["ln_a_b"]).reshape(1, D), "w_a_out": _chunk_cols(f(inp["w_a_out"])[0]),
        "ln_b_g": f(inp["ln_b_g"])[0], "ln_b_b": f(inp["ln_b_b"])[0], "w_s": f(inp["w_s"])[0],
        "b_s": f(inp["b_s"]).reshape(1, 8 * 128), "w_b_out": _chunk_cols(f(inp["w_b_out"])[0]), "w_out": _chunk_cols(f(inp["w_out"])[0]),
        "g_ffn": f(inp["g_ffn"]).reshape(1, D), "w_up": _chunk_cols(f(inp["w_up"])[0]), "dw_f": f(inp["dw_f"])[0],
        "b_dw_f": f(inp["b_dw_f"]).reshape(1, DFF), "w_down": _chunk_wdown(f(inp["w_down"])[0]), "g_final": f(inp["g_final"]),
    }
    xp, xsm = f(inp["x_prompt"]), f(inp["x_sample"])
    sa_, sf_ = f(inp["state_conv_a"])[0], f(inp["state_ffn_conv"])[0]
    in_maps = []
    for c in range(8):
        m = dict(shared)
        m["x_p"] = xp[c]
        m["x_s"] = xsm[16 * c:16 * c + 16].reshape(64, D)
        m["st_a"] = sa_[16 * c:16 * c + 16]
        m["st_f"] = sf_[16 * c:16 * c + 16].reshape(32, DFF)
        in_maps.append(m)
    res = run_bass_kernel_spmd(nc, in_maps, core_ids=list(range(8)))
    R = res.results
    y_p = np.stack([R[c]["o_yp"] for c in range(8)], 0)
    y_s = np.concatenate([R[c]["o_ys"].reshape(16, 4, D) for c in range(8)], 0)
    cap = np.stack([R[c]["o_cap"] for c in range(8)], 0)[None]
    cas = np.concatenate([R[c]["o_cas"] for c in range(8)], 0)[None]
    vs = np.concatenate([R[c]["o_vs"].reshape(16, 4, D) for c in range(8)], 0)[None]
    fp = np.stack([R[c]["o_fp"] for c in range(8)], 0)[None]
    fs = np.concatenate([R[c]["o_fs"].reshape(16, 2, DFF) for c in range(8)], 0)[None]
    return (y_p.astype(np.float32), y_s.astype(np.float32), cap.astype(np.float32), cas.astype(np.float32),
            vs.astype(np.float32), fp.astype(np.float32), fs.astype(np.float32))
```
